# Optimizing a Trainium2 kernel written in Bass

```python
import math
import jax, jax.numpy as jnp
from jax import lax
import numpy as np

D_MODEL = 1024
BATCH = 2
SEQ = 8192
DEPTH = 1

CHUNK = 64
Q_BLOCK = 128
N_MEM = 256
ROPE_THETA = 10000.0
EPS = 1e-5

DA_HEADS = 4
DA_HEAD_DIM = 64
DA_V_DIM = 2 * DA_HEAD_DIM
DA_WIDTH = DA_HEADS * DA_V_DIM

CONV_CH = 256
CONV_WIDTH = 31

MEM_HEADS = 4
MEM_HEAD_DIM = 64
MEM_WIDTH = MEM_HEADS * MEM_HEAD_DIM

MIX_WIDTH = DA_WIDTH + CONV_CH + MEM_WIDTH
IN_Q = DA_HEADS * 2 * DA_HEAD_DIM
IN_K = DA_HEADS * 2 * DA_HEAD_DIM
IN_V = DA_WIDTH
IN_CONV = 2 * CONV_CH
IN_MEMQ = MEM_WIDTH
IN_WIDTH = IN_Q + IN_K + IN_V + IN_CONV + IN_MEMQ

D_FF = 2816

kernel_name = "hybrid_diffattn_conformer_conv_memxattn_macaron"


def rmsnorm(x, g):
    xf = x.astype(jnp.float32)
    y = xf * lax.rsqrt(jnp.mean(xf * xf, axis=-1, keepdims=True) + EPS)
    return (y * g.astype(jnp.float32)).astype(x.dtype)


def layernorm(x, g, b):
    xf = x.astype(jnp.float32)
    mu = jnp.mean(xf, axis=-1, keepdims=True)
    var = jnp.mean(jnp.square(xf - mu), axis=-1, keepdims=True)
    y = (xf - mu) * lax.rsqrt(var + EPS)
    return (y * g.astype(jnp.float32) + b.astype(jnp.float32)).astype(x.dtype)


def swiglu(x, w_gate, w_up, w_down):
    return (jax.nn.silu(x @ w_gate) * (x @ w_up)) @ w_down


def rope_tables(seq, dim):
    inv_freq = 1.0 / (ROPE_THETA ** (jnp.arange(0, dim, 2, dtype=jnp.float32) / dim))
    ang = jnp.arange(seq, dtype=jnp.float32)[:, None] * inv_freq[None, :]
    return jnp.cos(ang), jnp.sin(ang)


def apply_rope(x, cos, sin):
    c = cos[None, :, None, None, :].astype(x.dtype)
    s = sin[None, :, None, None, :].astype(x.dtype)
    x1, x2 = jnp.split(x, 2, axis=-1)
    return jnp.concatenate([x1 * c - x2 * s, x2 * c + x1 * s], axis=-1)


def differential_attention(q, k, v, lam, subln_g, lam_init):
    b, s, h, _, d = q.shape
    nb = s // Q_BLOCK
    scale = d ** -0.5
    kf = k.astype(jnp.float32)
    vf = v.astype(jnp.float32)
    key_chunk = jnp.arange(s) // CHUNK
    qb = q.reshape(b, nb, Q_BLOCK, h, 2, d).transpose(1, 0, 2, 3, 4, 5)

    def block(args):
        q_blk, bi = args
        q_chunk = (bi * Q_BLOCK + jnp.arange(Q_BLOCK)) // CHUNK
        mask = key_chunk[None, :] <= q_chunk[:, None]
        sc = jnp.einsum('bqhcd,bkhcd->bhcqk', q_blk.astype(jnp.float32), kf) * scale
        sc = jnp.where(mask[None, None, None], sc, -jnp.inf)
        p = jax.nn.softmax(sc, axis=-1)
        a = p[:, :, 0] - lam * p[:, :, 1]
        return jnp.einsum('bhqk,bkhe->bqhe', a, vf)

    o = lax.map(block, (qb, jnp.arange(nb)))
    o = o.transpose(1, 0, 2, 3, 4).reshape(b, s, h, 2 * d)
    o = o * lax.rsqrt(jnp.mean(o * o, axis=-1, keepdims=True) + EPS)
    o = o * subln_g.astype(jnp.float32) * (1.0 - lam_init)
    return o.reshape(b, s, h * 2 * d).astype(q.dtype)


def conformer_conv(u, dw_w, dw_b, ln_g, ln_b):
    a, gate = jnp.split(u, 2, axis=-1)
    g = a * jax.nn.sigmoid(gate)
    c = g.shape[-1]
    y = lax.conv_general_dilated(
        g, dw_w[:, None, :].astype(g.dtype), window_strides=(1,),
        padding=[(CONV_WIDTH - 1, 0)], dimension_numbers=('NWC', 'WIO', 'NWC'),
        feature_group_count=c)
    y = y + dw_b
    y = layernorm(y, ln_g, ln_b)
    return jax.nn.silu(y)


def memory_cross_attention(q, mem_n, w_mem_kv):
    b, s, _ = q.shape
    kv = mem_n @ w_mem_kv
    mk, mv = jnp.split(kv, 2, axis=-1)
    qh = q.reshape(b, s, MEM_HEADS, MEM_HEAD_DIM).astype(jnp.float32)
    mk = mk.reshape(b, -1, MEM_HEADS, MEM_HEAD_DIM).astype(jnp.float32)
    mv = mv.reshape(b, -1, MEM_HEADS, MEM_HEAD_DIM).astype(jnp.float32)
    sc = jnp.einsum('bshd,bmhd->bhsm', qh, mk) * (MEM_HEAD_DIM ** -0.5)
    p = jax.nn.softmax(sc, axis=-1)
    o = jnp.einsum('bhsm,bmhd->bshd', p, mv)
    return o.reshape(b, s, MEM_WIDTH).astype(q.dtype)


def setup_inputs(seed: int = 0) -> dict:
    key = jax.random.key(seed)
    ks = jax.random.split(key, 32)
    L, D, F = DEPTH, D_MODEL, D_FF
    f32 = jnp.float32

    def nrm(k, shape, fan_in):
        return jax.random.normal(k, shape, f32) * (fan_in ** -0.5)

    def gain(k, shape):
        return 1.0 + 0.02 * jax.random.normal(k, shape, f32)

    def small(k, shape, s=0.02):
        return s * jax.random.normal(k, shape, f32)

    return {
        "x": jax.random.normal(ks[0], (BATCH, SEQ, D), f32),
        "mem": jax.random.normal(ks[1], (BATCH, N_MEM, D), f32),
        "ffn1_norm_g": gain(ks[2], (L, D)),
        "ffn1_w_gate": nrm(ks[3], (L, D, F), D),
        "ffn1_w_up": nrm(ks[4], (L, D, F), D),
        "ffn1_w_down": nrm(ks[5], (L, F, D), F),
        "mix_norm_g": gain(ks[6], (L, D)),
        "mem_norm_g": gain(ks[7], (L, D)),
        "w_in": nrm(ks[8], (L, D, IN_WIDTH), D),
        "lambda_q1": 0.1 * jax.random.normal(ks[9], (L, DA_HEAD_DIM), f32),
        "lambda_k1": 0.1 * jax.random.normal(ks[10], (L, DA_HEAD_DIM), f32),
        "lambda_q2": 0.1 * jax.random.normal(ks[11], (L, DA_HEAD_DIM), f32),
        "lambda_k2": 0.1 * jax.random.normal(ks[12], (L, DA_HEAD_DIM), f32),
        "subln_g": gain(ks[13], (L, DA_V_DIM)),
        "conv_dw_w": nrm(ks[14], (L, CONV_WIDTH, CONV_CH), CONV_WIDTH),
        "conv_dw_b": small(ks[15], (L, CONV_CH)),
        "conv_ln_g": gain(ks[16], (L, CONV_CH)),
        "conv_ln_b": small(ks[17], (L, CONV_CH)),
        "w_mem_kv": nrm(ks[18], (L, D, 2 * MEM_WIDTH), D),
        "w_out": nrm(ks[19], (L, MIX_WIDTH, D), MIX_WIDTH),
        "ffn2_norm_g": gain(ks[20], (L, D)),
        "ffn2_w_gate": nrm(ks[21], (L, D, F), D),
        "ffn2_w_up": nrm(ks[22], (L, D, F), D),
        "ffn2_w_down": nrm(ks[23], (L, F, D), F),
        "final_norm_g": gain(ks[24], (D,)),
    }


def reference(x, mem, ffn1_norm_g, ffn1_w_gate, ffn1_w_up, ffn1_w_down,
              mix_norm_g, mem_norm_g, w_in, lambda_q1, lambda_k1, lambda_q2, lambda_k2,
              subln_g, conv_dw_w, conv_dw_b, conv_ln_g, conv_ln_b, w_mem_kv, w_out,
              ffn2_norm_g, ffn2_w_gate, ffn2_w_up, ffn2_w_down, final_norm_g):
    b, s, _ = x.shape
    cos, sin = rope_tables(s, DA_HEAD_DIM)
    split_idx = np.cumsum([IN_Q, IN_K, IN_V, IN_CONV]).tolist()
    h = x
    for l in range(DEPTH):
        h = h + 0.5 * swiglu(rmsnorm(h, ffn1_norm_g[l]), ffn1_w_gate[l], ffn1_w_up[l], ffn1_w_down[l])

        n = rmsnorm(h, mix_norm_g[l])
        proj = n @ w_in[l]
        q, k, v, u_conv, q_mem = jnp.split(proj, split_idx, axis=-1)

        q = apply_rope(q.reshape(b, s, DA_HEADS, 2, DA_HEAD_DIM), cos, sin)
        k = apply_rope(k.reshape(b, s, DA_HEADS, 2, DA_HEAD_DIM), cos, sin)
        v = v.reshape(b, s, DA_HEADS, DA_V_DIM)
        lam_init = 0.8 - 0.6 * math.exp(-0.3 * l)
        lam = (jnp.exp(jnp.sum(lambda_q1[l] * lambda_k1[l]).astype(jnp.float32))
               - jnp.exp(jnp.sum(lambda_q2[l] * lambda_k2[l]).astype(jnp.float32)) + lam_init)
        o_da = differential_attention(q, k, v, lam, subln_g[l], lam_init)

        o_conv = conformer_conv(u_conv, conv_dw_w[l], conv_dw_b[l], conv_ln_g[l], conv_ln_b[l])

        o_mem = memory_cross_attention(q_mem, rmsnorm(mem, mem_norm_g[l]), w_mem_kv[l])

        mixed = jnp.concatenate([o_da, o_conv.astype(o_da.dtype), o_mem], axis=-1)
        h = h + mixed @ w_out[l]

        h = h + 0.5 * swiglu(rmsnorm(h, ffn2_norm_g[l]), ffn2_w_gate[l], ffn2_w_up[l], ffn2_w_down[l])
    return rmsnorm(h, final_norm_g)
```

```python
import numpy as np
from contextlib import ExitStack
import concourse.bass as bass
import concourse.mybir as mybir
from concourse.bass_utils import run_bass_kernel_spmd

F32 = mybir.dt.float32
BF16 = mybir.dt.bfloat16
AF = mybir.ActivationFunctionType
ALU = mybir.AluOpType

D = 1024
NDC = 8
F = 2816
NFC = 22
T = 2048
NTT = 4
NBLK = 16
SEQ = 8192
EPS = 1e-5
NCORES = 8
G = 4
KT_COLS = 4 * T
V_COLS = 4 * NBLK * 128
GT_COLS = 2 * NBLK * 32
BLOB_COLS = KT_COLS + V_COLS + GT_COLS
LAM_INIT = 0.8 - 0.6 * 1.0
AGS = ("agK0", "agK1", "agV0", "agV1", "agG")


class Slot:
    def __init__(self, sem):
        self.sem = sem
        self.count = 0


class Prog:
    ENGS = ("pe", "act", "dve", "pool", "sp")

    def __init__(self, nc, es):
        self.nc = nc
        self.es = es
        self.streams = {e: [] for e in self.ENGS}
        self.esem = {}
        self.cnt = {}
        for e in ("pe", "act", "dve", "pool"):
            self.esem[e] = Slot(es.enter_context(nc.semaphore("sem_" + e)))
        self.waited = {e: {} for e in self.ENGS}
        self.res = {}
        self.slots = {}
        self.nwaits = 0

    def slot(self, name):
        if name not in self.slots:
            self.slots[name] = Slot(self.es.enter_context(self.nc.semaphore("ds_" + name)))
        return self.slots[name]

    def _wait(self, eng, slot, val):
        w = self.waited[eng]
        key = id(slot)
        if w.get(key, 0) >= val:
            return
        w[key] = val
        sem = slot.sem
        self.nwaits += 1
        self.streams[eng].append(lambda e, sem=sem, val=val: e.wait_ge(sem, val))

    def op(self, eng, fn, reads=(), writes=(), dma=None, inc=True, dma_inc=16):
        deps = []
        for r in reads:
            st = self.res.get(r)
            if st and st[0] is not None:
                deps.append((st[0], "raw"))
        for w in writes:
            st = self.res.get(w)
            if st:
                if st[0] is not None:
                    deps.append((st[0], "waw"))
                for t in st[1]:
                    deps.append((t, "war"))
        for (slot, val, owner), kind in deps:
            if owner == eng and dma is None and kind != "raw" and eng == "pe":
                continue
            if owner == "dma":
                val = slot.count
            self._wait(eng, slot, val)
        if dma is not None:
            s = self.slot(dma)
            s.count += dma_inc
            ticket = (s, s.count, "dma")
            sem = s.sem
            if dma_inc == 16:
                self.streams[eng].append(lambda e, fn=fn, sem=sem: fn(e).then_inc(sem, 16))
            else:
                self.streams[eng].append(lambda e, fn=fn, sem=sem: fn(e).then_inc(sem))
        else:
            s = self.esem[eng]
            s.count += 1
            ticket = (s, s.count, eng)
            sem = s.sem
            self.streams[eng].append(lambda e, fn=fn, sem=sem: fn(e).then_inc(sem, 1))
        for r in reads:
            self.res.setdefault(r, [None, []])[1].append(ticket)
        for w in writes:
            self.res[w] = [ticket, []]
        return ticket

    def barrier(self, exclude=()):
        ex = [self.slots[n] for n in exclude if n in self.slots]
        for e in self.ENGS:
            for s in list(self.esem.values()) + list(self.slots.values()):
                if s.count and not any(s is x for x in ex):
                    self._wait(e, s, s.count)

    def emit(self):
        nc = self.nc
        st = self.streams
        with nc.Block() as block:
            @block.sync
            def _(e):
                for f in st["sp"]:
                    f(e)

            @block.gpsimd
            def _(e):
                for f in st["pool"]:
                    f(e)

            @block.scalar
            def _(e):
                for f in st["act"]:
                    f(e)

            @block.vector
            def _(e):
                for f in st["dve"]:
                    f(e)

            @block.tensor
            def _(e):
                for f in st["pe"]:
                    f(e)


def build(stage=63):
    nc = bass.Bass("TRN2", target_bir_lowering=False)
    es = ExitStack()
    with es:
        P = Prog(nc, es)

        def din(name, shape, dt=F32):
            return nc.dram_tensor(name, list(shape), dt, kind="ExternalInput").ap()

        xT = din("xT", [NDC, 128, T])
        outT = nc.dram_tensor("outT", [NDC, 128, T], F32, kind="ExternalOutput").ap()
        gains = din("gains", [128, 5, NDC])
        ident_d = din("ident", [128, 128])
        w_gate = [din(f"wg{i}", [NFC // 2, 128, NDC, 256]) for i in range(2)]
        w_up = [din(f"wu{i}", [NFC // 2, 128, NDC, 256]) for i in range(2)]
        w_down = [din(f"wd{i}", [NDC, 128, NFC, 128]) for i in range(2)]
        w_inf = din("w_inf", [11, 128, NDC, 256])
        w_inv = din("w_inv", [128, NDC, 512])
        w_kv = din("w_kv", [128, NDC, 512])
        w_o = din("w_o", [4, 128, NDC, 256])
        ropeC = din("ropeC", [128, T])
        ropeS = din("ropeS", [128, T])
        memT = din("memT", [NDC, 128, 256])
        smalls = din("smalls", [128, 96])
        lamv = din("lamv", [128, 4, 64])
        maskd = din("maskd", [4, 128, 128])
        def dint(name, shape):
            return nc.dram_tensor(name, list(shape), BF16, kind="Internal").ap()
        bKi = [dint(f"bKi{i}", [128, 2 * T]) for i in range(2)]
        bKo = [dint(f"bKo{i}", [G * 128, 2 * T]) for i in range(2)]
        bVi = [dint(f"bVi{i}", [128, 2 * T]) for i in range(2)]
        bVo = [dint(f"bVo{i}", [G * 128, 2 * T]) for i in range(2)]
        bGi = dint("bGi", [128, GT_COLS])
        bGo = dint("bGo", [G * 128, GT_COLS])

        def sb(name, shape, dt=F32):
            return es.enter_context(nc.sbuf_tensor(name, list(shape), dt))

        hT = sb("hT", [128, NDC, T])
        actT = sb("actT", [128, NDC, T], BF16)
        gains_s = sb("gains_s", [128, 5, NDC])
        smalls_s = sb("smalls_s", [128, 96])
        ident = sb("ident_s", [128, 128])
        ident_bf = sb("ident_bf", [128, 128], BF16)
        ones_bf = sb("ones_bf", [128, 128], BF16)
        ones256 = sb("ones256", [128, 128], BF16)
        lam_s = sb("lam_s", [128, 4, 64])
        lam_t = sb("lam_t", [128, 2, 64])
        lam_r = sb("lam_r", [128, 4])
        psS = [es.enter_context(nc.psum_tensor(f"psS{i}", [128, 1024], F32)) for i in range(2)]
        ps = [psS[0][:, 0:512], psS[0][:, 512:1024], psS[1][:, 0:512], psS[1][:, 512:1024]] + \
             [es.enter_context(nc.psum_tensor(f"ps{i}", [128, 512], F32)) for i in range(4, 8)]
        bank_rr = [0]

        def nbank():
            b = bank_rr[0]
            bank_rr[0] = (b + 1) % 8
            return b

        rr = {}

        def rot(name, n):
            v = rr.get(name, 0)
            rr[name] = (v + 1) % n
            return v

        def tcols(tt):
            return slice(tt * 512, (tt + 1) * 512)

        def v3(ap, b=128):
            return ap.rearrange("p (a b) -> p a b", b=b)

        P.op("sp", lambda e: e.dma_start(out=gains_s[:], in_=gains), writes=["gains"], dma="c0")
        P.op("sp", lambda e: e.dma_start(out=smalls_s[:], in_=smalls), writes=["smalls"], dma="c1")
        P.op("sp", lambda e: e.dma_start(out=ident[:], in_=ident_d), writes=["ident"], dma="c2")
        P.op("sp", lambda e: e.dma_start(out=lam_s[:], in_=lamv), writes=["lam_s"], dma="c3")
        P.op("dve", lambda e: e.memset(ones_bf[:], 1.0 / 1024.0), writes=["ones"])
        P.op("dve", lambda e: e.memset(ones256[:], 1.0 / 256.0), writes=["ones256"])
        P.op("dve", lambda e: e.tensor_copy(out=ident_bf[:], in_=ident[:]), reads=["ident"], writes=["ident_bf"])
        P.op("dve", lambda e: e.tensor_tensor(out=lam_t[:], in0=lam_s[:, 0:2, :], in1=lam_s[:, 2:4, :], op=ALU.mult),
             reads=["lam_s"], writes=["lam_t"])
        P.op("dve", lambda e: e.reduce_sum(out=lam_r[:, 0:2], in_=lam_t[:], axis=mybir.AxisListType.X),
             reads=["lam_t"], writes=["lam_r01"])
        P.op("act", lambda e: e.activation(out=lam_r[:, 2:4], in_=lam_r[:, 0:2], func=AF.Exp),
             reads=["lam_r01"], writes=["lam_r23"])
        P.op("dve", lambda e: e.tensor_tensor(out=lam_r[:, 0:1], in0=lam_r[:, 3:4], in1=lam_r[:, 2:3], op=ALU.subtract),
             reads=["lam_r23", "lam_r01"], writes=["lam_r0"])
        P.op("dve", lambda e: e.tensor_scalar_add(out=smalls_s[:, 74:75], in0=lam_r[:, 0:1], scalar1=-LAM_INIT),
             reads=["lam_r0", "smalls"], writes=["neglam"])
        P.op("dve", lambda e: e.tensor_scalar_mul(out=smalls_s[:, 75:76], in0=smalls_s[:, 68:69], scalar1=1.0 - LAM_INIT),
             reads=["smalls"], writes=["sg8"])

        P.op("dve", lambda e: e.memset(smalls_s[:, 76:77], EPS), reads=["smalls"], writes=["epsc"])

        def rsqrt_act(dst, src, scale, reads, writes):
            P.op("act", lambda e: e.activation(out=dst, in_=src, func=AF.Ln, scale=scale, bias=smalls_s[:, 76:77]),
                 reads=list(reads) + ["epsc"], writes=list(writes))
            P.op("act", lambda e: e.activation(out=dst, in_=dst, func=AF.Exp, scale=-0.5),
                 reads=list(writes), writes=list(writes))

        def rmsnorm(scr, gi, dst_fn, dst_res_fn, after_fn=None, tts=range(NTT)):
            sq, rstd = scr
            for tt in tts:
                for dc in range(NDC):
                    P.op("act", lambda e, dc=dc, tt=tt: e.activation(
                        out=sq[:, dc, :], in_=hT[:, dc, tcols(tt)], func=AF.Square),
                        reads=[("hT", dc, tt)], writes=[("sq", dc)])
                b = nbank()

                def mm(e, b=b):
                    for dc in range(NDC):
                        i = e.matmul(ps[b][:, :], lhsT=ones_bf[:, :], rhs=sq[:, dc, :],
                                     start=(dc == 0), stop=(dc == NDC - 1))
                    return i
                P.op("pe", mm, reads=[("sq", dc) for dc in range(NDC)] + ["ones"], writes=[("ps", b)])
                r = rot("rstd", 2)
                rsqrt_act(rstd[r][:, :], ps[b][:, :], 1.0, [("ps", b)], [("rstd", r)])
                for dc in range(NDC):
                    dst, dres = dst_fn(dc, tt), dst_res_fn(dc, tt)
                    P.op("dve", lambda e, dc=dc, tt=tt, r=r, dst=dst: e.scalar_tensor_tensor(
                        out=dst, in0=hT[:, dc, tcols(tt)],
                        scalar=gains_s[:, gi, dc:dc + 1], in1=rstd[r][:, :],
                        op0=ALU.mult, op1=ALU.mult),
                        reads=[("hT", dc, tt), ("rstd", r), "gains"], writes=[dres])
                    if after_fn is not None:
                        after_fn(dc, tt, dres)

        def act_dst(dc, tt):
            return actT[:, dc, tcols(tt)]

        def act_res(dc, tt):
            return ("act", dc, tt)

        def load_w(buf_list, name, src, dst_fn=None):
            s = rot(name, len(buf_list))
            dst = buf_list[s][:] if dst_fn is None else dst_fn(buf_list[s])
            P.op("pool", lambda e, dst=dst: e.dma_start(out=dst, in_=src),
                 writes=[(name, s)], dma=f"{name}{s}")
            return s

        def proj(wbuf, wres, sub, tt, b):
            def mm(e):
                for dc in range(NDC):
                    i = e.matmul(ps[b][:, :], lhsT=wbuf[:, dc, sub * 128:(sub + 1) * 128],
                                 rhs=actT[:, dc, tcols(tt)], start=(dc == 0), stop=(dc == NDC - 1))
                return i
            P.op("pe", mm, reads=[wres] + [("act", dc, tt) for dc in range(NDC)], writes=[("ps", b)])

        def ffn(li, gi, HT, wA, wB, wD, sq, rstd, tmpf, after_half=None):
            rmsnorm((sq, rstd), gi, act_dst, act_res, tts=(0, 1))
            for half in range(2):
                tts = (2 * half, 2 * half + 1)
                for fcp in range(NFC // 2):
                    if half == 0 and fcp == 2:
                        rmsnorm((sq, rstd), gi, act_dst, act_res, tts=(2, 3))
                    sa = load_w(wA, "wA", w_gate[li][fcp])
                    sbb = load_w(wB, "wB", w_up[li][fcp])
                    for sub in range(2):
                        fc = 2 * fcp + sub
                        for tt in tts:
                            tl = tt - 2 * half
                            bg, bu = nbank(), nbank()
                            proj(wA[sa], ("wA", sa), sub, tt, bg)
                            proj(wB[sbb], ("wB", sbb), sub, tt, bu)
                            k = rot("tmpf", 4)
                            P.op("act", lambda e, k=k, bg=bg: e.activation(
                                out=tmpf[k][:, :], in_=ps[bg][:, :], func=AF.Silu),
                                reads=[("ps", bg)], writes=[("tmpf", k)])
                            P.op("dve", lambda e, k=k, bu=bu, fc=fc, tl=tl: e.tensor_tensor(
                                out=HT[:, fc, tl * 512:(tl + 1) * 512], in0=ps[bu][:, :], in1=tmpf[k][:, :],
                                op=ALU.mult), reads=[("ps", bu), ("tmpf", k)], writes=[("HT", fc, tl)])
                for dc in range(NDC):
                    s = load_w(wD, "wD", w_down[li][dc])
                    for tt in tts:
                        tl = tt - 2 * half
                        b = nbank()

                        def mm(e, s=s, tl=tl, b=b):
                            for fc in range(NFC):
                                i = e.matmul(ps[b][:, :], lhsT=wD[s][:, fc, :], rhs=HT[:, fc, tl * 512:(tl + 1) * 512],
                                             start=(fc == 0), stop=(fc == NFC - 1))
                            return i
                        P.op("pe", mm, reads=[("wD", s)] + [("HT", fc, tl) for fc in range(NFC)],
                             writes=[("ps", b)])
                        P.op("dve", lambda e, b=b, dc=dc, tt=tt: e.scalar_tensor_tensor(
                            out=hT[:, dc, tcols(tt)], in0=ps[b][:, :], scalar=0.5, in1=hT[:, dc, tcols(tt)],
                            op0=ALU.mult, op1=ALU.add), reads=[("ps", b), ("hT", dc, tt)],
                            writes=[("hT", dc, tt)])
                if after_half is not None:
                    after_half(tts, sq, rstd)

        ffn_calls = [0]

        def ffn_phase(li, gi, final=False):
            ffn_calls[0] += 1
            u_ = ffn_calls[0]
            with ExitStack() as ph:
                def psb(name, shape, dt=F32):
                    return ph.enter_context(nc.sbuf_tensor(name, list(shape), dt))
                HT = psb(f"ffnHT{li}_{u_}", [128, NFC, 1024], BF16)
                wA = [psb(f"fwA{li}_{i}_{u_}", [128, NDC, 256], BF16) for i in range(3)]
                wB = [psb(f"fwB{li}_{i}_{u_}", [128, NDC, 256], BF16) for i in range(3)]
                wD = [psb(f"fwD{li}_{i}_{u_}", [128, NFC, 128], BF16) for i in range(2)]
                fsq = psb(f"fsq{li}_{u_}", [128, NDC, 512], BF16)
                frstd = [psb(f"frstd{li}_{i}_{u_}", [128, 512]) for i in range(2)]
                ftmpf = [psb(f"ftmpf{li}_{i}_{u_}", [128, 512]) for i in range(4)]
                if final:
                    ost = [psb(f"ost{i}", [128, 512]) for i in range(2)]

                    def fin_store(dc, tt, dres):
                        o = dres[1]
                        P.op("sp", lambda e, dc=dc, tt=tt, o=o: e.dma_start(out=outT[dc][:, tcols(tt)], in_=ost[o][:, :]),
                             reads=[dres], dma=f"out{o}")

                    def after_half(tts, sq, rstd):
                        rmsnorm((sq, rstd), 3, lambda dc, tt: ost[(dc + tt * NDC) % 2][:, :],
                                lambda dc, tt: ("ost", (dc + tt * NDC) % 2), after_fn=fin_store, tts=tts)
                    ffn(li, gi, HT, wA, wB, wD, fsq, frstd, ftmpf, after_half=after_half)
                else:
                    ffn(li, gi, HT, wA, wB, wD, fsq, frstd, ftmpf)
                P.barrier()

        def proj_phase(v, own, QT, qmT, g_ext):
          with ExitStack() as ph:
            def psb(name, shape, dt=F32):
                return ph.enter_context(nc.sbuf_tensor(name, list(shape), dt))
            ropeC_s = psb(f"ropeC_s_{v}", [128, T])
            ropeS_s = psb(f"ropeS_s_{v}", [128, T])
            wA = [psb(f"mwA{i}_{v}", [128, NDC, 256], BF16) for i in range(2)]
            wB = [psb(f"mwB{i}_{v}", [128, NDC, 256], BF16) for i in range(2)]
            wV = psb(f"mwV_{v}", [128, NDC, 512], BF16)
            kst = [psb(f"kst{i}_{v}", [128, T], BF16) for i in range(1)]
            p_sq = psb(f"p_sq_{v}", [128, NDC, 512], BF16)
            p_rstd = [psb(f"p_rstd{i}_{v}", [128, 512]) for i in range(2)]
            p_tmpf = [psb(f"p_tmpf{i}_{v}", [128, 512]) for i in range(4)]
            gtail = psb(f"gtail_{v}", [128, 2, NBLK, 32], BF16)
            P.op("sp", lambda e: e.dma_start(out=ropeC_s[:], in_=ropeC), writes=["ropeC"], dma="rc")
            P.op("sp", lambda e: e.dma_start(out=ropeS_s[:], in_=ropeS), writes=["ropeS"], dma="rs")
            P.op("pool", lambda e: e.dma_start(out=wV[:], in_=w_inv), writes=["wV"], dma="wv")
            rmsnorm((p_sq, p_rstd), 1, act_dst, act_res)

            ag_pending = []

            def flush_ag():
                while ag_pending:
                    hp_ = ag_pending.pop(0)
                    P.op("pool", lambda e, hp_=hp_: e.collective_compute(
                        "AllGather", ALU.bypass, replica_groups=[[0, 1, 2, 3], [4, 5, 6, 7]],
                        ins=[bKi[hp_].opt()], outs=[bKo[hp_].opt()]),
                        reads=[("bK", 2 * hp_), ("bK", 2 * hp_ + 1)], writes=[("bKo", hp_)], dma=f"agK{hp_}", dma_inc=1)

            def rope_group(base_pair, is_k, pre=None):
                for hp in range(2):
                    if pre is not None and hp == 0:
                        sa, sbb = pre
                    else:
                        sa = load_w(wA, "wA", w_inf[base_pair + hp])
                        sbb = load_w(wB, "wB", w_inf[base_pair + 2 + hp])
                    flush_ag()
                    for sub in range(2):
                        h = 2 * hp + sub
                        ks = 0 if is_k else None
                        for tt in range(NTT):
                            ba, bb = nbank(), nbank()
                            proj(wA[sa], ("wA", sa), sub, tt, ba)
                            proj(wB[sbb], ("wB", sbb), sub, tt, bb)
                            k1, k2 = rot("tmpf", 4), rot("tmpf", 4)
                            P.op("dve", lambda e, k1=k1, ba=ba, tt=tt: e.tensor_tensor(
                                out=p_tmpf[k1][:, :], in0=ps[ba][:, :], in1=ropeC_s[:, tcols(tt)], op=ALU.mult),
                                reads=[("ps", ba), "ropeC"], writes=[("tmpf", k1)])
                            P.op("dve", lambda e, k2=k2, bb=bb, tt=tt: e.tensor_tensor(
                                out=p_tmpf[k2][:, :], in0=ps[bb][:, :], in1=ropeS_s[:, tcols(tt)], op=ALU.mult),
                                reads=[("ps", bb), "ropeS"], writes=[("tmpf", k2)])
                            if is_k:
                                dst, dres = kst[ks][:, tcols(tt)], ("kst", ks, tt)
                            else:
                                dst, dres = QT[:, h, tcols(tt)], ("QT", h, tt)
                            P.op("dve", lambda e, k1=k1, k2=k2, dst=dst: e.tensor_tensor(
                                out=dst, in0=p_tmpf[k1][:, :], in1=p_tmpf[k2][:, :], op=ALU.add),
                                reads=[("tmpf", k1), ("tmpf", k2)], writes=[dres])
                        if is_k:
                            P.op("sp", lambda e, ks=ks, h=h: e.dma_start(
                                out=bKi[h // 2][:, (h % 2) * T:(h % 2 + 1) * T], in_=kst[ks][:, :]),
                                reads=[("kst", ks, tt) for tt in range(NTT)], writes=[("bK", h)], dma=f"kst{ks}")
                            if h % 2 == 1 and hp == 0:
                                ag_pending.append(hp)
            rope_group(4, True)
            flush_ag()
            glu_pre = (load_w(wA, "wA", w_inf[8]), load_w(wB, "wB", w_inf[9]))
            bVx = [bVi[i].rearrange("p (h x) -> p h x", h=2) for i in range(2)]
            vst = [p_sq[:, 0:4, :], p_sq[:, 4:8, :]]
            vst4 = [v_.rearrange("p h (m e) -> p h m e", e=128) for v_ in vst]
            for tt in range(NTT):
                s = tt % 2
                for mi in range(4):
                    m = 4 * tt + mi
                    b = nbank()

                    def mm(e, m=m, b=b):
                        for dc in range(NDC):
                            i = e.matmul(ps[b][:, :], lhsT=actT[:, dc, m * 128:(m + 1) * 128], rhs=wV[:, dc, :],
                                         start=(dc == 0), stop=(dc == NDC - 1))
                        return i
                    P.op("pe", mm, reads=["wV"] + [("act", dc, tt) for dc in range(NDC)], writes=[("ps", b)])
                    P.op("act", lambda e, b=b, s=s, mi=mi: e.activation(out=vst4[s][:, :, mi, :], in_=v3(ps[b][:, :]), func=AF.Copy),
                         reads=[("ps", b)], writes=[("vst", s, mi)] + [("sq", dc) for dc in range(NDC)])
                for i_ in range(2):
                    P.op("sp", lambda e, s=s, tt=tt, i_=i_: e.dma_start(
                        out=bVx[i_][:, :, tt * 512:(tt + 1) * 512], in_=vst[s][:, 2 * i_:2 * i_ + 2, :]),
                        reads=[("vst", s, mi) for mi in range(4)], writes=[("bV", tt, i_)], dma=f"vst{s}")
            if own:
                q_pre = (load_w(wA, "wA", w_inf[0]), load_w(wB, "wB", w_inf[2]))
            for i_ in range(1):
                P.op("pool", lambda e, i_=i_: e.collective_compute(
                    "AllGather", ALU.bypass, replica_groups=[[0, 1, 2, 3], [4, 5, 6, 7]], ins=[bVi[i_].opt()], outs=[bVo[i_].opt()]),
                    reads=[("bV", tt, i_) for tt in range(NTT)], writes=[("bVo", i_)], dma=f"agV{i_}", dma_inc=1)
            sa, sbb = glu_pre
            for c in range(2):
                for tt in range(NTT):
                    ba, bb = nbank(), nbank()
                    proj(wA[sa], ("wA", sa), c, tt, ba)
                    proj(wB[sbb], ("wB", sbb), c, tt, bb)
                    k = rot("tmpf", 4)
                    P.op("act", lambda e, k=k, bb=bb: e.activation(
                        out=p_tmpf[k][:, :], in_=ps[bb][:, :], func=AF.Sigmoid),
                        reads=[("ps", bb)], writes=[("tmpf", k)])
                    P.op("dve", lambda e, k=k, ba=ba, c=c, tt=tt: e.tensor_tensor(
                        out=g_ext[:, c, 4 * tt:4 * tt + 4, 32:160], in0=v3(ps[ba][:, :]), in1=v3(p_tmpf[k][:, :]),
                        op=ALU.mult), reads=[("ps", ba), ("tmpf", k)], writes=[("g", c, tt)])
                P.op("dve", lambda e, c=c: e.tensor_copy(out=gtail[:, c, :, :], in_=g_ext[:, c, :, 128:160]),
                     reads=[("g", c, tt) for tt in range(NTT)], writes=[("gtail", c)])
            P.op("sp", lambda e: e.dma_start(
                out=bGi, in_=gtail[:].rearrange("p c m t -> p (c m t)")),
                reads=[("gtail", 0), ("gtail", 1)], writes=["bG"], dma="gt")
            P.op("pool", lambda e: e.collective_compute(
                "AllGather", ALU.bypass, replica_groups=[[0, 1, 2, 3], [4, 5, 6, 7]], ins=[bGi.opt()], outs=[bGo.opt()]),
                reads=["bG"], writes=["bGo"], dma="agG", dma_inc=1)
            if own:
                rope_group(0, False, pre=q_pre)
            if own:
                sa = load_w(wA, "wA", w_inf[10])
                for sub in range(2):
                    for tt in range(NTT):
                        b = nbank()
                        proj(wA[sa], ("wA", sa), sub, tt, b)
                        P.op("act", lambda e, b=b, sub=sub, tt=tt: e.activation(
                            out=qmT[:, sub, tcols(tt)], in_=ps[b][:, :], func=AF.Copy),
                            reads=[("ps", b)], writes=[("qmT", sub, tt)])
            P.barrier(exclude=AGS)


        def load_x(v):
            for tt in range(NTT):
                for dc in range(NDC):
                    P.op("sp", lambda e, dc=dc, tt=tt: e.dma_start(out=hT[:, dc, tcols(tt)], in_=xT[dc][:, tcols(tt)]),
                         writes=[("hT", dc, tt)], dma=f"x_t{tt}")
                if tt == 0:
                    s0 = P.slot("x_t0")
                    P._wait("sp", s0, s0.count)

        load_x(0)
        if stage & 1:
            ffn_phase(0, 0)

        if stage & 2:
          with ExitStack() as mid1:
            QT = mid1.enter_context(nc.sbuf_tensor("QT", [128, 4, T], BF16))
            with ExitStack() as mid2:
                qmT = mid2.enter_context(nc.sbuf_tensor("qmT", [128, 2, T], BF16))
                g_ext = mid2.enter_context(nc.sbuf_tensor("g_ext", [128, 2, NBLK, 160], BF16))
                proj_phase(0, True, QT, qmT, g_ext)
                with ExitStack() as ph:
                    def psb(name, shape, dt=F32):
                        return ph.enter_context(nc.sbuf_tensor(name, list(shape), dt))
                    mem_s = psb("mem_s", [128, NDC, 256])
                    m_sq = psb("m_sq", [128, NDC, 256], BF16)
                    m_rstd = psb("m_rstd", [128, 256])
                    memn = psb("memn", [128, NDC, 256], BF16)
                    wkv = psb("wkv", [128, NDC, 512], BF16)
                    mkT = psb("mkT", [128, 2, 256], BF16)
                    mv = psb("mv", [128, 2, 4, 66], BF16)
                    pm = [psb(f"pm{i}", [128, 512], BF16) for i in range(4)]
                    om = [psb(f"om{i}", [128, 256]) for i in range(2)]
                    rl = [psb(f"rl{i}", [128, 4]) for i in range(2)]
                    tl_s = psb("tl_s", [128, 4, 2 * NBLK * 32], BF16)
                    hal = psb("hal", [128, 2, NBLK, 32])
                    dg = psb("dg", [128, 62, 128], BF16)
                    ysb = [psb(f"ysb{i}", [128, 512]) for i in range(2)]
                    ybf = [psb(f"ybf{i}", [128, 512], BF16) for i in range(2)]
                    ysq = [psb(f"ysq{i}", [128, 512], BF16) for i in range(2)]
                    mu_s = psb("mu_s", [128, 512])
                    var_s = psb("var_s", [128, 512])
                    if stage & 8:
                        for i in range(62):
                            eng = "dve"
                            P.op(eng, lambda e, i=i: e.tensor_scalar(
                                out=dg[:, i, :], in0=ident_bf[:, :], scalar1=smalls_s[:, i:i + 1], scalar2=None, op0=ALU.mult),
                                reads=["ident_bf", "smalls"], writes=[("dg", i)])

                    if stage & 4:
                        P.op("sp", lambda e: e.dma_start(out=mem_s[:], in_=memT.rearrange("c p t -> p c t")),
                             writes=["mem_s"], dma="mem")
                        P.op("pool", lambda e: e.dma_start(out=wkv[:], in_=w_kv), writes=["wkv"], dma="wkv")
                        for dc in range(NDC):
                            P.op("act", lambda e, dc=dc: e.activation(out=m_sq[:, dc, :], in_=mem_s[:, dc, :], func=AF.Square),
                                 reads=["mem_s"], writes=[("sq", dc)])
                        b = nbank()

                        def mm(e, b=b):
                            for dc in range(NDC):
                                i = e.matmul(ps[b][:, 0:256], lhsT=ones_bf[:, :], rhs=m_sq[:, dc, :],
                                             start=(dc == 0), stop=(dc == NDC - 1))
                            return i
                        P.op("pe", mm, reads=[("sq", dc) for dc in range(NDC)] + ["ones"], writes=[("ps", b)])
                        rsqrt_act(m_rstd[:, :], ps[b][:, 0:256], 1.0, [("ps", b)], [("rstd", 0)])
                        for dc in range(NDC):
                            P.op("dve", lambda e, dc=dc: e.scalar_tensor_tensor(
                                out=memn[:, dc, :], in0=mem_s[:, dc, :], scalar=gains_s[:, 4, dc:dc + 1],
                                in1=m_rstd[:, :], op0=ALU.mult, op1=ALU.mult),
                                reads=["mem_s", ("rstd", 0), "gains"], writes=[("memn", dc)])
                        memn_res = [("memn", dc) for dc in range(NDC)]
                        for c2 in range(2):
                            b = nbank()

                            def mm(e, c2=c2, b=b):
                                for dc in range(NDC):
                                    i = e.matmul(ps[b][:, 0:256], lhsT=wkv[:, dc, c2 * 128:(c2 + 1) * 128], rhs=memn[:, dc, :],
                                                 start=(dc == 0), stop=(dc == NDC - 1))
                                return i
                            P.op("pe", mm, reads=["wkv"] + memn_res, writes=[("ps", b)])
                            P.op("act", lambda e, c2=c2, b=b: e.activation(out=mkT[:, c2, :], in_=ps[b][:, 0:256], func=AF.Copy),
                                 reads=[("ps", b)], writes=[("mkT", c2)])
                        P.op("dve", lambda e: e.memset(mv[:], 1.0), writes=["mv0", "mv1"])
                        for mb in range(2):
                            b = nbank()

                            def mm(e, mb=mb, b=b):
                                for dc in range(NDC):
                                    i = e.matmul(ps[b][:, 0:256], lhsT=memn[:, dc, mb * 128:(mb + 1) * 128], rhs=wkv[:, dc, 256:512],
                                                 start=(dc == 0), stop=(dc == NDC - 1))
                                return i
                            P.op("pe", mm, reads=["wkv"] + memn_res, writes=[("ps", b)])
                            P.op("act", lambda e, mb=mb, b=b: e.activation(
                                out=mv[:, mb, :, 0:64], in_=v3(ps[b][:, 0:256], 64), func=AF.Copy),
                                reads=[("ps", b), f"mv{mb}"], writes=[f"mv{mb}"])
                        for tt in range(NTT):
                            bO = (0, 1, 2, 3)

                            def acc(qb, bO=bO):
                                return v3(ps[bO[qb]][:, 0:264], 66)
                            for h in range(4):
                                c2, base = h // 2, (h % 2) * 64
                                pk = []
                                for mb in range(2):
                                    bS = 4 + rot("mbank", 4)
                                    P.op("pe", lambda e, bS=bS, c2=c2, base=base, mb=mb, tt=tt: e.matmul(
                                        ps[bS][:, :], lhsT=mkT[base:base + 64, c2, mb * 128:(mb + 1) * 128],
                                        rhs=qmT[base:base + 64, c2, tcols(tt)], start=True, stop=True),
                                        reads=[("mkT", c2), ("qmT", c2, tt)], writes=[("ps", bS)])
                                    k = rot("pm", 4)
                                    pk.append(k)
                                    P.op("act", lambda e, k=k, bS=bS: e.activation(
                                        out=pm[k][:, :], in_=ps[bS][:, :], func=AF.Exp, scale=0.125),
                                        reads=[("ps", bS)], writes=[("pm", k)])

                                def mm(e, h=h, pk=tuple(pk), acc=acc):
                                    for qb in range(4):
                                        for mb in range(2):
                                            i = e.matmul(acc(qb)[:, h, 0:66], lhsT=pm[pk[mb]][:, qb * 128:(qb + 1) * 128],
                                                         rhs=mv[:, mb, h, 0:66], start=(mb == 0), stop=(mb == 1))
                                    return i
                                P.op("pe", mm, reads=[("pm", pk[0]), ("pm", pk[1]), "mv0", "mv1"],
                                     writes=[("ps", bO[i_]) for i_ in range(4)])
                            for qb in range(4):
                                o = rot("om", 2)
                                P.op("dve", lambda e, o=o, qb=qb, acc=acc: e.reciprocal(out=rl[o][:, :], in_=acc(qb)[:, :, 64]),
                                     reads=[("ps", bO[qb])], writes=[("rl", o)])
                                for h in range(4):
                                    P.op("dve", lambda e, o=o, qb=qb, h=h, acc=acc: e.tensor_scalar(
                                        out=om[o][:, h * 64:(h + 1) * 64], in0=acc(qb)[:, h, 0:64], scalar1=rl[o][:, h:h + 1],
                                        scalar2=None, op0=ALU.mult),
                                        reads=[("ps", bO[qb]), ("rl", o)], writes=[("om", o, h)])
                                bT = 4 + rot("mbank", 4)

                                def tr(e, o=o, bT=bT):
                                    for c2 in range(2):
                                        i = e.transpose(out=ps[bT][:, c2 * 128:(c2 + 1) * 128], in_=om[o][:, c2 * 128:(c2 + 1) * 128],
                                                        identity=ident[:, :])
                                    return i
                                P.op("pe", tr, reads=[("om", o, h) for h in range(4)] + ["ident"], writes=[("ps", bT)])
                                cs = slice(tt * 512 + qb * 128, tt * 512 + (qb + 1) * 128)
                                P.op("act", lambda e, bT=bT, cs=cs: e.activation(
                                    out=actT[:, 6:8, cs], in_=v3(ps[bT][:, 0:256]), func=AF.Copy),
                                    reads=[("ps", bT)], writes=[("act", 6, tt), ("act", 7, tt)])
                    else:
                        P.op("dve", lambda e: e.memset(actT[:, 6:8, :], 0.0),
                             writes=[("act", d_, t_) for d_ in (6, 7) for t_ in range(NTT)])
                    if stage & 8:
                        for r in range(4):
                            P.op("sp", lambda e, r=r: e.dma_start(
                                out=tl_s[:, r, :], in_=bGo[r * 128:(r + 1) * 128, :]),
                                reads=["bGo"], writes=[("tl", r)], dma="tl")
                        tl5 = tl_s[:].rearrange("p r (c m t) -> p r c m t", c=2, m=NBLK)
                        for c in range(2):
                            P.op("dve", lambda e, c=c: e.tensor_scalar(
                                out=hal[:, c], in0=tl5[:, 0, c], scalar1=smalls_s[:, 80:81], scalar2=None, op0=ALU.mult),
                                reads=[("tl", 0), "smalls"], writes=[("hal", c)])
                            for r in (1, 2, 3):
                                P.op("dve", lambda e, c=c, r=r: e.scalar_tensor_tensor(
                                    out=hal[:, c], in0=tl5[:, r, c], scalar=smalls_s[:, 80 + r:81 + r], in1=hal[:, c],
                                    op0=ALU.mult, op1=ALU.add), reads=[("tl", r), ("hal", c)], writes=[("hal", c)])
                            for r in range(4):
                                P.op("dve", lambda e, c=c, r=r: e.scalar_tensor_tensor(
                                    out=hal[:, c, 1:NBLK, :], in0=tl5[:, r, c, 0:NBLK - 1, :], scalar=smalls_s[:, 84 + r:85 + r],
                                    in1=hal[:, c, 1:NBLK, :], op0=ALU.mult, op1=ALU.add),
                                    reads=[("tl", r), ("hal", c)], writes=[("hal", c)])
                            P.op("dve", lambda e, c=c: e.tensor_copy(out=g_ext[:, c, :, 0:32], in_=hal[:, c]),
                                 reads=[("hal", c)], writes=[("gh", c)])
                        for tt in range(NTT):
                            for c in range(2):
                                b = nbank()

                                def mm(e, c=c, tt=tt, b=b):
                                    for w in range(31):
                                        i = e.matmul(v3(ps[b][:, :]), lhsT=dg[:, c * 31 + w, :],
                                                     rhs=g_ext[:, c, 4 * tt:4 * tt + 4, w + 2:w + 130],
                                                     start=(w == 0), stop=(w == 30))
                                    return i
                                P.op("pe", mm, reads=[("dg", c * 31 + w) for w in range(31)] + [("g", c, t_) for t_ in range(NTT)] + [("gh", c)],
                                     writes=[("ps", b)])
                                P.op("act", lambda e, c=c, b=b: e.activation(
                                    out=ysb[c][:, :], in_=ps[b][:, :], func=AF.Identity, bias=smalls_s[:, 62 + c:63 + c]),
                                    reads=[("ps", b), "smalls"], writes=[("ysb", c)])
                                P.op("act", lambda e, c=c, b=b: e.activation(
                                    out=ysq[c][:, :], in_=ps[b][:, :], func=AF.Square, bias=smalls_s[:, 62 + c:63 + c]),
                                    reads=[("ps", b), "smalls"], writes=[("ysq", c)])
                                P.op("dve", lambda e, c=c: e.tensor_copy(out=ybf[c][:, :], in_=ysb[c][:, :]),
                                     reads=[("ysb", c)], writes=[("ybf", c)])
                            bmu, bm2 = nbank(), nbank()

                            def mm(e, bmu=bmu, bm2=bm2):
                                for c in range(2):
                                    e.matmul(ps[bmu][:, :], lhsT=ones256[:, :], rhs=ybf[c][:, :], start=(c == 0), stop=(c == 1))
                                for c in range(2):
                                    i = e.matmul(ps[bm2][:, :], lhsT=ones256[:, :], rhs=ysq[c][:, :], start=(c == 0), stop=(c == 1))
                                return i
                            P.op("pe", mm, reads=[("ybf", 0), ("ybf", 1), ("ysq", 0), ("ysq", 1), "ones256"],
                                 writes=[("ps", bmu), ("ps", bm2)])
                            P.op("act", lambda e, bmu=bmu: e.activation(out=mu_s[:, :], in_=ps[bmu][:, :], func=AF.Copy),
                                 reads=[("ps", bmu)], writes=["mu_s"])
                            P.op("dve", lambda e: e.tensor_tensor(out=var_s[:, :], in0=mu_s[:, :], in1=mu_s[:, :], op=ALU.mult),
                                 reads=["mu_s"], writes=["var_s"])
                            P.op("dve", lambda e, bm2=bm2: e.tensor_tensor(out=var_s[:, :], in0=ps[bm2][:, :], in1=var_s[:, :], op=ALU.subtract),
                                 reads=[("ps", bm2), "var_s"], writes=["var_s"])
                            rsqrt_act(var_s[:, :], var_s[:, :], 1.0, ["var_s"], ["var_s"])
                            for c in range(2):
                                P.op("dve", lambda e, c=c: e.tensor_tensor(out=ysb[c][:, :], in0=ysb[c][:, :], in1=mu_s[:, :], op=ALU.subtract),
                                     reads=[("ysb", c), "mu_s", ("ybf", c)], writes=[("ysb", c)])
                                P.op("dve", lambda e, c=c: e.tensor_tensor(out=ysb[c][:, :], in0=ysb[c][:, :], in1=var_s[:, :], op=ALU.mult),
                                     reads=[("ysb", c), "var_s"], writes=[("ysb", c)])
                                P.op("act", lambda e, c=c, tt=tt: e.activation(
                                    out=actT[:, 4 + c, tcols(tt)], in_=ysb[c][:, :], func=AF.Silu,
                                    scale=smalls_s[:, 64 + c:65 + c], bias=smalls_s[:, 66 + c:67 + c]),
                                    reads=[("ysb", c), "smalls"], writes=[("act", 4 + c, tt)])
                    else:
                        P.op("dve", lambda e: e.memset(actT[:, 4:6, :], 0.0),
                             writes=[("act", d_, t_) for d_ in (4, 5) for t_ in range(NTT)])
                    P.barrier()

            with ExitStack() as ph:
                def psb(name, shape, dt=F32):
                    return ph.enter_context(nc.sbuf_tensor(name, list(shape), dt))
                if stage & 16:
                    KT = [psb(f"KT{i}", [128, 4, T], BF16) for i in range(2)]
                    VA = [psb(f"VA{i}", [128, 4 * NBLK, 132], BF16) for i in range(2)]
                    PT = [psb(f"PT{i}", [128, 2, 512], BF16) for i in range(3)]
                    maskb = psb("maskb", [128, 4, 128], BF16)
                    Osb = [psb(f"Osb{i}", [128, 8, 130]) for i in range(2)]
                    cmb = [psb(f"cmb{i}", [128, 8]) for i in range(4)]
                    ot = [psb(f"ot{i}", [128, 128]) for i in range(4)]
                    oo = [psb(f"oo{i}", [128, 128]) for i in range(4)]
                    osq = [psb(f"osq{i}", [128, 128]) for i in range(4)]
                    P.op("pool", lambda e: e.dma_start(out=maskb[:], in_=maskd.rearrange("r k q -> k r q")),
                         writes=["maskb"], dma="mask")
                    for s_ in range(2):
                        P.op("dve", lambda e, s_=s_: e.memset(VA[s_][:, :, 128:130], 1.0), writes=[("va1", s_)])
                    zr = psb("zr", [128, 512], BF16)
                    P.op("dve", lambda e: e.memset(zr[:], 0.0), writes=["zr"])

                    def zero_acc(e):
                        for bi, n in ((0, 3), (1, 3), (2, 2)):
                            i = e.matmul(ps[4 + bi][:, 0:130 * n], lhsT=zr[:, 0:128], rhs=zr[:, 0:130 * n], start=True, stop=False)
                        return i

                    def accO(i):
                        return ps[4 + i // 3][:, (i % 3) * 130:(i % 3) * 130 + 130]
                    pending = []

                    def flush_pending():
                        for f in pending:
                            f()
                        pending.clear()
                    for h in range(4):
                        s = h % 2
                        for r in range(4):
                            P.op("sp", lambda e, s=s, r=r, h=h: e.dma_start(
                                out=KT[s][:, r, :], in_=bKo[h // 2][r * 128:(r + 1) * 128, (h % 2) * T:(h % 2 + 1) * T]),
                                reads=[("bKo", h // 2)], writes=[("kt", s, r)], dma=f"kt{s}_{r}")
                            P.op("sp", lambda e, s=s, r=r, h=h: e.dma_start(
                                out=VA[s][:, r * NBLK:(r + 1) * NBLK, 0:128],
                                in_=v3(bVo[h // 2][r * 128:(r + 1) * 128, (h % 2) * T:(h % 2 + 1) * T])),
                                reads=[("bVo", h // 2)], writes=[("va", s, r)], dma=f"va{s}_{r}")
                        if h == 0:
                            P.op("pool", lambda e: e.collective_compute(
                                "AllGather", ALU.bypass, replica_groups=[[0, 1, 2, 3], [4, 5, 6, 7]],
                                ins=[bKi[1].opt()], outs=[bKo[1].opt()]),
                                reads=[("bK", 2), ("bK", 3)] + [("kt", 0, r_) for r_ in range(4)] + [("va", 0, r_) for r_ in range(4)],
                                writes=[("bKo", 1)], dma="agK1", dma_inc=1)
                            P.op("pool", lambda e: e.collective_compute(
                                "AllGather", ALU.bypass, replica_groups=[[0, 1, 2, 3], [4, 5, 6, 7]],
                                ins=[bVi[1].opt()], outs=[bVo[1].opt()]),
                                reads=[("bV", tt_, 1) for tt_ in range(NTT)], writes=[("bVo", 1)], dma="agV1", dma_inc=1)
                        for qg in range(4):
                            pairs = [(r, mp) for r in range(4) for mp in range(4 * qg + 4)]
                            npairs = len(pairs)
                            info = []

                            def emit_qk(idx, r, mp):
                                spb = idx % 2
                                c0 = max(mp - 4 * qg, 0)

                                def mm(e, spb=spb, c0=c0, r=r, mp=mp, s=s, h=h, qg=qg):
                                    for comp in range(2):
                                        i = e.matmul(ps[2 * spb + comp][:, c0 * 128:512],
                                                     lhsT=KT[s][comp * 64:(comp + 1) * 64, r, mp * 128:(mp + 1) * 128],
                                                     rhs=QT[comp * 64:(comp + 1) * 64, h, qg * 512 + c0 * 128:(qg + 1) * 512],
                                                     start=True, stop=True)
                                    return i
                                P.op("pe", mm, reads=[("kt", s, r), ("QT", h, qg)], writes=[("ps", 2 * spb), ("ps", 2 * spb + 1)])
                                k = rot("PT", 3)

                                def ex(e, spb=spb, c0=c0, k=k):
                                    return e.activation(out=PT[k][:, :, c0 * 128:512],
                                                        in_=psS[spb][:, :].rearrange("p (c n) -> p c n", c=2)[:, :, c0 * 128:512],
                                                        func=AF.Exp, scale=0.125)
                                P.op("act", ex, reads=[("ps", 2 * spb), ("ps", 2 * spb + 1)], writes=[("PT", k)])
                                if mp >= 4 * qg:
                                    def mk(e, c0=c0, k=k, r=r):
                                        for comp in range(2):
                                            i = e.tensor_tensor(out=PT[k][:, comp, c0 * 128:(c0 + 1) * 128],
                                                                in0=PT[k][:, comp, c0 * 128:(c0 + 1) * 128],
                                                                in1=maskb[:, r, :], op=ALU.mult)
                                        return i
                                    P.op("dve", mk, reads=[("PT", k), "maskb"], writes=[("PT", k)])
                                return (k, c0)

                            def emit_pv(idx, r, mp, k, c0):
                                def mm(e, r=r, mp=mp, k=k, c0=c0, qg=qg, s=s):
                                    for qb in range(c0, 4):
                                        for comp in range(2):
                                            i = e.matmul(accO(2 * qb + comp), lhsT=PT[k][:, comp, qb * 128:(qb + 1) * 128],
                                                         rhs=VA[s][:, r * NBLK + mp, 0:130],
                                                         start=False,
                                                         stop=(r == 3 and mp == 4 * qg + qb and (qb, comp) in ((1, 0), (2, 1), (3, 1))))
                                    return i
                                P.op("pe", mm, reads=[("PT", k), ("va", s, r), ("va1", s)],
                                     writes=[("ps", 4), ("ps", 5), ("ps", 6)])
                            prev = None
                            P.op("pe", zero_acc, reads=["zr"], writes=[("ps", 4), ("ps", 5), ("ps", 6)])
                            for idx, (r, mp) in enumerate(pairs):
                                cur = emit_qk(idx, r, mp)
                                if prev is not None:
                                    emit_pv(*prev)
                                prev = (idx, r, mp, cur[0], cur[1])
                                if idx == 3:
                                    flush_pending()
                            emit_pv(*prev)
                            flush_pending()
                            osl = rot("Osb", 2)
                            for bi, n in ((0, 3), (1, 3), (2, 2)):
                                if bi == 1:
                                    P.op("act", lambda e, bi=bi, n=n, osl=osl: e.activation(
                                        out=Osb[osl][:, 3 * bi:3 * bi + n, :], in_=v3(ps[4 + bi][:, 0:130 * n], 130), func=AF.Copy),
                                        reads=[("ps", 4 + bi)], writes=[("Osb", osl, bi)])
                                else:
                                    P.op("dve", lambda e, bi=bi, n=n, osl=osl: e.tensor_copy(
                                        out=Osb[osl][:, 3 * bi:3 * bi + n, :], in_=v3(ps[4 + bi][:, 0:130 * n], 130)),
                                        reads=[("ps", 4 + bi)], writes=[("Osb", osl, bi)])
                            ores = [("Osb", osl, bi) for bi in range(3)]
                            for qb in range(4):
                                o = qb
                                O = Osb[osl]
                                c8 = cmb[o]
                                P.op("dve", lambda e, O=O, c8=c8, qb=qb: e.reciprocal(out=c8[:, 0:2], in_=O[:, 2 * qb:2 * qb + 2, 128]),
                                     reads=ores, writes=[("cmb", o)])
                                P.op("dve", lambda e, c8=c8: e.tensor_scalar(out=c8[:, 2:3], in0=c8[:, 1:2], scalar1=smalls_s[:, 74:75],
                                                                             scalar2=None, op0=ALU.mult),
                                     reads=[("cmb", o), "neglam"], writes=[("cmb", o)])
                                P.op("dve", lambda e, O=O, c8=c8, qb=qb, o=o: e.tensor_scalar(
                                    out=ot[o][:, :], in0=O[:, 2 * qb, 0:128], scalar1=c8[:, 0:1], scalar2=None, op0=ALU.mult),
                                    reads=ores + [("cmb", o)], writes=[("ot", o)])
                                P.op("dve", lambda e, O=O, c8=c8, qb=qb, o=o: e.scalar_tensor_tensor(
                                    out=oo[o][:, :], in0=O[:, 2 * qb + 1, 0:128], scalar=c8[:, 2:3], in1=ot[o][:, :],
                                    op0=ALU.mult, op1=ALU.add), reads=ores + [("cmb", o), ("ot", o)], writes=[("oo", o)])
                                P.op("dve", lambda e, o=o: e.tensor_tensor(out=osq[o][:, :], in0=oo[o][:, :], in1=oo[o][:, :], op=ALU.mult),
                                     reads=[("oo", o)], writes=[("osq", o)])
                                P.op("dve", lambda e, o=o, c8=c8: e.reduce_sum(out=c8[:, 3:4], in_=osq[o][:, :], axis=mybir.AxisListType.X),
                                     reads=[("osq", o), ("cmb", o)], writes=[("cmb", o)])
                                def fin(o=o, qb=qb, h=h, qg=qg, c8=c8):
                                    rsqrt_act(c8[:, 5:6], c8[:, 3:4], 1.0 / 128.0, [("cmb", o)], [("cmb5", o)])
                                    P.op("dve", lambda e: e.tensor_scalar(out=ot[o][:, :], in0=oo[o][:, :], scalar1=c8[:, 5:6],
                                                                          scalar2=None, op0=ALU.mult),
                                         reads=[("oo", o), ("cmb5", o), ("ot", o)], writes=[("ot", o)])
                                    P.op("pe", lambda e: e.transpose(out=ps[7][:, qb * 128:(qb + 1) * 128], in_=ot[o][:, :],
                                                                    identity=ident[:, :]),
                                         reads=[("ot", o), "ident"], writes=[("ps7", qb)])
                                    P.op("dve", lambda e: e.tensor_scalar(
                                        out=actT[:, h, qg * 512 + qb * 128:qg * 512 + (qb + 1) * 128],
                                        in0=ps[7][:, qb * 128:(qb + 1) * 128], scalar1=smalls_s[:, 75:76], scalar2=None, op0=ALU.mult),
                                        reads=[("ps7", qb), "sg8"], writes=[("act", h, qg)])
                                pending.append(fin)
                    flush_pending()
                else:
                    P.op("dve", lambda e: e.memset(actT[:, 0:4, :], 0.0),
                         writes=[("act", d_, t_) for d_ in range(4) for t_ in range(NTT)])
                P.barrier()

            with ExitStack() as ph:
                wA = [ph.enter_context(nc.sbuf_tensor(f"owA{i}", [128, NDC, 256], BF16)) for i in range(2)]
                for dcp in range(4):
                    sa = load_w(wA, "wA", w_o[dcp])
                    for sub in range(2):
                        dc = 2 * dcp + sub
                        for tt in range(NTT):
                            b = nbank()
                            proj(wA[sa], ("wA", sa), sub, tt, b)
                            P.op("dve", lambda e, b=b, dc=dc, tt=tt: e.tensor_tensor(
                                out=hT[:, dc, tcols(tt)], in0=ps[b][:, :], in1=hT[:, dc, tcols(tt)], op=ALU.add),
                                reads=[("ps", b), ("hT", dc, tt)], writes=[("hT", dc, tt)])
                P.barrier()

        if stage & 32:
            ffn_phase(1, 2, final=True)
        else:
            f_sq = sb("f_sq", [128, NDC, 512], BF16)
            f_rstd = [sb(f"f_rstd{i}", [128, 512]) for i in range(2)]
            ost = [sb(f"ost{i}", [128, 512]) for i in range(2)]

            def fin_store(dc, tt, dres):
                o = dres[1]
                P.op("sp", lambda e, dc=dc, tt=tt, o=o: e.dma_start(out=outT[dc][:, tcols(tt)], in_=ost[o][:, :]),
                     reads=[dres], dma=f"out{o}")
            rmsnorm((f_sq, f_rstd), 3, lambda dc, tt: ost[(dc + tt * NDC) % 2][:, :],
                    lambda dc, tt: ("ost", (dc + tt * NDC) % 2), after_fn=fin_store)
        P.barrier()
        P.emit()
    return nc


def _tok_idx(j):
    m = np.arange(NBLK)
    g = 4 * m + j
    return (g[:, None] * 128 + np.arange(128)[None, :]).reshape(-1)


def _pairs(w, ncol_chunks):
    return np.ascontiguousarray(
        w.reshape(NDC, 128, ncol_chunks, 256).transpose(2, 1, 0, 3))


def _prep(inputs):
    f = lambda k: np.asarray(inputs[k], dtype=np.float32)
    x = f("x")
    mem = f("mem")
    common = {}
    for i, pre in enumerate(("ffn1", "ffn2")):
        common[f"wg{i}"] = _pairs(f(pre + "_w_gate")[0], NFC // 2)
        common[f"wu{i}"] = _pairs(f(pre + "_w_up")[0], NFC // 2)
        wd = f(pre + "_w_down")[0]
        common[f"wd{i}"] = np.ascontiguousarray(
            wd.reshape(NFC, 128, NDC, 128).transpose(2, 1, 0, 3))
    w_in = f("w_in")[0]
    q, k, v, u, qm = (w_in[:, 0:512], w_in[:, 512:1024], w_in[:, 1024:1536],
                      w_in[:, 1536:2048], w_in[:, 2048:2304])

    def swap(w):
        w4 = w.reshape(D, 4, 2, 2, 32)
        return w4[:, :, :, ::-1, :].reshape(D, 512)
    wf = np.concatenate([q, swap(q), k, swap(k), u, qm], axis=1)
    assert wf.shape[1] == 11 * 256
    common["w_inf"] = _pairs(np.ascontiguousarray(wf), wf.shape[1] // 256)
    common["w_inv"] = np.ascontiguousarray(v.reshape(NDC, 128, 512).transpose(1, 0, 2))
    common["w_kv"] = np.ascontiguousarray(f("w_mem_kv")[0].reshape(NDC, 128, 512).transpose(1, 0, 2))
    common["w_o"] = _pairs(f("w_out")[0], 4)
    gains = np.stack([f("ffn1_norm_g")[0], f("mix_norm_g")[0], f("ffn2_norm_g")[0],
                      f("final_norm_g"), f("mem_norm_g")[0]], 0)
    common["gains"] = np.ascontiguousarray(gains.reshape(5, NDC, 128).transpose(2, 0, 1))
    common["ident"] = np.eye(128, dtype=np.float32)
    sm = np.zeros((128, 96), np.float32)
    dw = f("conv_dw_w")[0]
    sm[:, 0:62] = dw.reshape(31, 2, 128).transpose(2, 1, 0).reshape(128, 62)
    sm[:, 62:64] = f("conv_dw_b")[0].reshape(2, 128).T
    sm[:, 64:66] = f("conv_ln_g")[0].reshape(2, 128).T
    sm[:, 66:68] = f("conv_ln_b")[0].reshape(2, 128).T
    sm[:, 68] = f("subln_g")[0]
    common["smalls_base"] = sm
    lam = np.stack([f("lambda_q1")[0], f("lambda_q2")[0], f("lambda_k1")[0], f("lambda_k2")[0]], 0)
    common["lamv"] = np.ascontiguousarray(np.broadcast_to(lam[None], (128, 4, 64))).astype(np.float32)

    inv_freq = (1.0 / (np.float32(10000.0) ** (np.arange(0, 64, 2, dtype=np.float32) / np.float32(64)))).astype(np.float32)
    in_maps = []
    for c in range(NCORES):
        b, j = c // 4, c % 4
        m = dict((k2, v2) for k2, v2 in common.items() if k2 != "smalls_base")
        idx = _tok_idx(j)
        m["xT"] = np.ascontiguousarray(x[b][idx].T.reshape(NDC, 128, T))
        ang = (idx.astype(np.float32)[:, None] * inv_freq[None, :]).astype(np.float32)
        cos = np.cos(ang).astype(np.float32).T
        sin = np.sin(ang).astype(np.float32).T
        m["ropeC"] = np.ascontiguousarray(np.concatenate([cos, cos, cos, cos], 0))
        m["ropeS"] = np.ascontiguousarray(np.concatenate([-sin, sin, -sin, sin], 0))
        m["memT"] = np.ascontiguousarray(mem[b].T.reshape(NDC, 128, 256))
        sm2 = common["smalls_base"].copy()
        rs = (j - 1) % 4
        if j == 0:
            sm2[:, 84 + rs] = 1.0
        else:
            sm2[:, 80 + rs] = 1.0
        m["smalls"] = sm2
        md = np.zeros((4, 128, 128), np.float32)
        for r in range(4):
            if r < j:
                md[r] = 1.0
            elif r == j:
                kk = np.arange(128)[:, None]
                qq = np.arange(128)[None, :]
                md[r] = ((kk < 64) | (qq >= 64)).astype(np.float32)
        m["maskd"] = md
        in_maps.append(m)
    return in_maps


_NC_CACHE = {}


def kernel(**inputs):
    stage = int(inputs.pop("_stage", 63)) if "_stage" in inputs else 63
    in_maps = _prep(inputs)
    if stage not in _NC_CACHE:
        _NC_CACHE[stage] = build(stage)
    nc = _NC_CACHE[stage]
    res = run_bass_kernel_spmd(nc, in_maps, core_ids=list(range(NCORES)))
    out = np.zeros((2, SEQ, D), np.float32)
    for c in range(NCORES):
        b, j = c // 4, c % 4
        o = np.asarray(res.results[c]["outT"]).reshape(D, T)
        out[b][_tok_idx(j)] = o.T
    return out
```

```python
import numpy as np
from contextlib import ExitStack
import concourse.bass as bass
import concourse.mybir as mybir
from concourse.bass_utils import run_bass_kernel_spmd

F32 = mybir.dt.float32
BF16 = mybir.dt.bfloat16
AF = mybir.ActivationFunctionType
ALU = mybir.AluOpType

D = 1024
NDC = 8
F = 2816
NFC = 22
T = 2048
NTT = 4
NBLK = 16
SEQ = 8192
EPS = 1e-5
NCORES = 8
G = 4
KT_COLS = 4 * T
V_COLS = 4 * NBLK * 128
GT_COLS = 2 * NBLK * 32
BLOB_COLS = KT_COLS + V_COLS + GT_COLS
LAM_INIT = 0.8 - 0.6 * 1.0
AGS = ("agK0", "agK1", "agV0", "agV1", "agG")


class Slot:
    def __init__(self, sem):
        self.sem = sem
        self.count = 0


class Prog:
    ENGS = ("pe", "act", "dve", "pool", "sp")

    def __init__(self, nc, es):
        self.nc = nc
        self.es = es
        self.streams = {e: [] for e in self.ENGS}
        self.esem = {}
        self.cnt = {}
        for e in ("pe", "act", "dve", "pool"):
            self.esem[e] = Slot(es.enter_context(nc.semaphore("sem_" + e)))
        self.waited = {e: {} for e in self.ENGS}
        self.res = {}
        self.slots = {}
        self.nwaits = 0

    def slot(self, name):
        if name not in self.slots:
            self.slots[name] = Slot(self.es.enter_context(self.nc.semaphore("ds_" + name)))
        return self.slots[name]

    def _wait(self, eng, slot, val):
        w = self.waited[eng]
        key = id(slot)
        if w.get(key, 0) >= val:
            return
        w[key] = val
        sem = slot.sem
        self.nwaits += 1
        self.streams[eng].append(lambda e, sem=sem, val=val: e.wait_ge(sem, val))

    def op(self, eng, fn, reads=(), writes=(), dma=None, inc=True, dma_inc=16):
        deps = []
        for r in reads:
            st = self.res.get(r)
            if st and st[0] is not None:
                deps.append((st[0], "raw"))
        for w in writes:
            st = self.res.get(w)
            if st:
                if st[0] is not None:
                    deps.append((st[0], "waw"))
                for t in st[1]:
                    deps.append((t, "war"))
        for (slot, val, owner), kind in deps:
            if owner == eng and dma is None and kind != "raw" and eng == "pe":
                continue
            if owner == "dma":
                val = slot.count
            self._wait(eng, slot, val)
        if dma is not None:
            s = self.slot(dma)
            s.count += dma_inc
            ticket = (s, s.count, "dma")
            sem = s.sem
            if dma_inc == 16:
                self.streams[eng].append(lambda e, fn=fn, sem=sem: fn(e).then_inc(sem, 16))
            else:
                self.streams[eng].append(lambda e, fn=fn, sem=sem: fn(e).then_inc(sem))
        else:
            s = self.esem[eng]
            s.count += 1
            ticket = (s, s.count, eng)
            sem = s.sem
            self.streams[eng].append(lambda e, fn=fn, sem=sem: fn(e).then_inc(sem, 1))
        for r in reads:
            self.res.setdefault(r, [None, []])[1].append(ticket)
        for w in writes:
            self.res[w] = [ticket, []]
        return ticket

    def barrier(self, exclude=()):
        ex = [self.slots[n] for n in exclude if n in self.slots]
        for e in self.ENGS:
            for s in list(self.esem.values()) + list(self.slots.values()):
                if s.count and not any(s is x for x in ex):
                    self._wait(e, s, s.count)

    def emit(self):
        nc = self.nc
        st = self.streams
        with nc.Block() as block:
            @block.sync
            def _(e):
                for f in st["sp"]:
                    f(e)

            @block.gpsimd
            def _(e):
                for f in st["pool"]:
                    f(e)

            @block.scalar
            def _(e):
                for f in st["act"]:
                    f(e)

            @block.vector
            def _(e):
                for f in st["dve"]:
                    f(e)

            @block.tensor
            def _(e):
                for f in st["pe"]:
                    f(e)


def build(stage=63):
    nc = bass.Bass("TRN2", target_bir_lowering=False)
    es = ExitStack()
    with es:
        P = Prog(nc, es)

        def din(name, shape, dt=F32):
            return nc.dram_tensor(name, list(shape), dt, kind="ExternalInput").ap()

        xT = din("xT", [NDC, 128, T])
        outT = nc.dram_tensor("outT", [NDC, 128, T], F32, kind="ExternalOutput").ap()
        gains = din("gains", [128, 5, NDC])
        ident_d = din("ident", [128, 128])
        w_gate = [din(f"wg{i}", [NFC // 2, 128, NDC, 256]) for i in range(2)]
        w_up = [din(f"wu{i}", [NFC // 2, 128, NDC, 256]) for i in range(2)]
        w_down = [din(f"wd{i}", [NDC, 128, NFC, 128]) for i in range(2)]
        w_inf = din("w_inf", [11, 128, NDC, 256])
        w_inv = din("w_inv", [128, NDC, 512])
        w_kv = din("w_kv", [128, NDC, 512])
        w_o = din("w_o", [4, 128, NDC, 256])
        ropeC = din("ropeC", [128, T])
        ropeS = din("ropeS", [128, T])
        memT = din("memT", [NDC, 128, 256])
        smalls = din("smalls", [128, 96])
        lamv = din("lamv", [128, 4, 64])
        maskd = din("maskd", [4, 128, 128])
        def dint(name, shape):
            return nc.dram_tensor(name, list(shape), BF16, kind="Internal").ap()
        bKi = [dint(f"bKi{i}", [128, 2 * T]) for i in range(2)]
        bKo = [dint(f"bKo{i}", [G * 128, 2 * T]) for i in range(2)]
        bVi = [dint(f"bVi{i}", [128, 2 * T]) for i in range(2)]
        bVo = [dint(f"bVo{i}", [G * 128, 2 * T]) for i in range(2)]
        bGi = dint("bGi", [128, GT_COLS])
        bGo = dint("bGo", [G * 128, GT_COLS])

        def sb(name, shape, dt=F32):
            return es.enter_context(nc.sbuf_tensor(name, list(shape), dt))

        hT = sb("hT", [128, NDC, T])
        actT = sb("actT", [128, NDC, T], BF16)
        gains_s = sb("gains_s", [128, 5, NDC])
        smalls_s = sb("smalls_s", [128, 96])
        ident = sb("ident_s", [128, 128])
        ident_bf = sb("ident_bf", [128, 128], BF16)
        ones_bf = sb("ones_bf", [128, 128], BF16)
        ones256 = sb("ones256", [128, 128], BF16)
        lam_s = sb("lam_s", [128, 4, 64])
        lam_t = sb("lam_t", [128, 2, 64])
        lam_r = sb("lam_r", [128, 4])
        psS = [es.enter_context(nc.psum_tensor(f"psS{i}", [128, 1024], F32)) for i in range(2)]
        ps = [psS[0][:, 0:512], psS[0][:, 512:1024], psS[1][:, 0:512], psS[1][:, 512:1024]] + \
             [es.enter_context(nc.psum_tensor(f"ps{i}", [128, 512], F32)) for i in range(4, 8)]
        bank_rr = [0]

        def nbank():
            b = bank_rr[0]
            bank_rr[0] = (b + 1) % 8
            return b

        rr = {}

        def rot(name, n):
            v = rr.get(name, 0)
            rr[name] = (v + 1) % n
            return v

        def tcols(tt):
            return slice(tt * 512, (tt + 1) * 512)

        def v3(ap, b=128):
            return ap.rearrange("p (a b) -> p a b", b=b)

        P.op("sp", lambda e: e.dma_start(out=gains_s[:], in_=gains), writes=["gains"], dma="c0")
        P.op("sp", lambda e: e.dma_start(out=smalls_s[:], in_=smalls), writes=["smalls"], dma="c1")
        P.op("sp", lambda e: e.dma_start(out=ident[:], in_=ident_d), writes=["ident"], dma="c2")
        P.op("sp", lambda e: e.dma_start(out=lam_s[:], in_=lamv), writes=["lam_s"], dma="c3")
        P.op("dve", lambda e: e.memset(ones_bf[:], 1.0 / 1024.0), writes=["ones"])
        P.op("dve", lambda e: e.memset(ones256[:], 1.0 / 256.0), writes=["ones256"])
        P.op("dve", lambda e: e.tensor_copy(out=ident_bf[:], in_=ident[:]), reads=["ident"], writes=["ident_bf"])
        P.op("dve", lambda e: e.tensor_tensor(out=lam_t[:], in0=lam_s[:, 0:2, :], in1=lam_s[:, 2:4, :], op=ALU.mult),
             reads=["lam_s"], writes=["lam_t"])
        P.op("dve", lambda e: e.reduce_sum(out=lam_r[:, 0:2], in_=lam_t[:], axis=mybir.AxisListType.X),
             reads=["lam_t"], writes=["lam_r01"])
        P.op("act", lambda e: e.activation(out=lam_r[:, 2:4], in_=lam_r[:, 0:2], func=AF.Exp),
             reads=["lam_r01"], writes=["lam_r23"])
        P.op("dve", lambda e: e.tensor_tensor(out=lam_r[:, 0:1], in0=lam_r[:, 3:4], in1=lam_r[:, 2:3], op=ALU.subtract),
             reads=["lam_r23", "lam_r01"], writes=["lam_r0"])
        P.op("dve", lambda e: e.tensor_scalar_add(out=smalls_s[:, 74:75], in0=lam_r[:, 0:1], scalar1=-LAM_INIT),
             reads=["lam_r0", "smalls"], writes=["neglam"])
        P.op("dve", lambda e: e.tensor_scalar_mul(out=smalls_s[:, 75:76], in0=smalls_s[:, 68:69], scalar1=1.0 - LAM_INIT),
             reads=["smalls"], writes=["sg8"])

        P.op("dve", lambda e: e.memset(smalls_s[:, 76:77], EPS), reads=["smalls"], writes=["epsc"])

        def rsqrt_act(dst, src, scale, reads, writes):
            P.op("act", lambda e: e.activation(out=dst, in_=src, func=AF.Ln, scale=scale, bias=smalls_s[:, 76:77]),
                 reads=list(reads) + ["epsc"], writes=list(writes))
            P.op("act", lambda e: e.activation(out=dst, in_=dst, func=AF.Exp, scale=-0.5),
                 reads=list(writes), writes=list(writes))

        def rmsnorm(scr, gi, dst_fn, dst_res_fn, after_fn=None, tts=range(NTT)):
            sq, rstd = scr
            for tt in tts:
                for dc in range(NDC):
                    P.op("act", lambda e, dc=dc, tt=tt: e.activation(
                        out=sq[:, dc, :], in_=hT[:, dc, tcols(tt)], func=AF.Square),
                        reads=[("hT", dc, tt)], writes=[("sq", dc)])
                b = nbank()

                def mm(e, b=b):
                    for dc in range(NDC):
                        i = e.matmul(ps[b][:, :], lhsT=ones_bf[:, :], rhs=sq[:, dc, :],
                                     start=(dc == 0), stop=(dc == NDC - 1))
                    return i
                P.op("pe", mm, reads=[("sq", dc) for dc in range(NDC)] + ["ones"], writes=[("ps", b)])
                r = rot("rstd", 2)
                rsqrt_act(rstd[r][:, :], ps[b][:, :], 1.0, [("ps", b)], [("rstd", r)])
                for dc in range(NDC):
                    dst, dres = dst_fn(dc, tt), dst_res_fn(dc, tt)
                    P.op("dve", lambda e, dc=dc, tt=tt, r=r, dst=dst: e.scalar_tensor_tensor(
                        out=dst, in0=hT[:, dc, tcols(tt)],
                        scalar=gains_s[:, gi, dc:dc + 1], in1=rstd[r][:, :],
                        op0=ALU.mult, op1=ALU.mult),
                        reads=[("hT", dc, tt), ("rstd", r), "gains"], writes=[dres])
                    if after_fn is not None:
                        after_fn(dc, tt, dres)

        def act_dst(dc, tt):
            return actT[:, dc, tcols(tt)]

        def act_res(dc, tt):
            return ("act", dc, tt)

        def load_w(buf_list, name, src, dst_fn=None):
            s = rot(name, len(buf_list))
            dst = buf_list[s][:] if dst_fn is None else dst_fn(buf_list[s])
            P.op("pool", lambda e, dst=dst: e.dma_start(out=dst, in_=src),
                 writes=[(name, s)], dma=f"{name}{s}")
            return s

        def proj(wbuf, wres, sub, tt, b):
            def mm(e):
                for dc in range(NDC):
                    i = e.matmul(ps[b][:, :], lhsT=wbuf[:, dc, sub * 128:(sub + 1) * 128],
                                 rhs=actT[:, dc, tcols(tt)], start=(dc == 0), stop=(dc == NDC - 1))
                return i
            P.op("pe", mm, reads=[wres] + [("act", dc, tt) for dc in range(NDC)], writes=[("ps", b)])

        def ffn(li, gi, HT, wA, wB, wD, sq, rstd, tmpf, after_half=None):
            rmsnorm((sq, rstd), gi, act_dst, act_res, tts=(0, 1))
            for half in range(2):
                tts = (2 * half, 2 * half + 1)
                for fcp in range(NFC // 2):
                    if half == 0 and fcp == 2:
                        rmsnorm((sq, rstd), gi, act_dst, act_res, tts=(2, 3))
                    sa = load_w(wA, "wA", w_gate[li][fcp])
                    sbb = load_w(wB, "wB", w_up[li][fcp])
                    for sub in range(2):
                        fc = 2 * fcp + sub
                        for tt in tts:
                            tl = tt - 2 * half
                            bg, bu = nbank(), nbank()
                            proj(wA[sa], ("wA", sa), sub, tt, bg)
                            proj(wB[sbb], ("wB", sbb), sub, tt, bu)
                            k = rot("tmpf", 4)
                            P.op("act", lambda e, k=k, bg=bg: e.activation(
                                out=tmpf[k][:, :], in_=ps[bg][:, :], func=AF.Silu),
                                reads=[("ps", bg)], writes=[("tmpf", k)])
                            P.op("dve", lambda e, k=k, bu=bu, fc=fc, tl=tl: e.tensor_tensor(
                                out=HT[:, fc, tl * 512:(tl + 1) * 512], in0=ps[bu][:, :], in1=tmpf[k][:, :],
                                op=ALU.mult), reads=[("ps", bu), ("tmpf", k)], writes=[("HT", fc, tl)])
                for dc in range(NDC):
                    s = load_w(wD, "wD", w_down[li][dc])
                    for tt in tts:
                        tl = tt - 2 * half
                        b = nbank()

                        def mm(e, s=s, tl=tl, b=b):
                            for fc in range(NFC):
                                i = e.matmul(ps[b][:, :], lhsT=wD[s][:, fc, :], rhs=HT[:, fc, tl * 512:(tl + 1) * 512],
                                             start=(fc == 0), stop=(fc == NFC - 1))
                            return i
                        P.op("pe", mm, reads=[("wD", s)] + [("HT", fc, tl) for fc in range(NFC)],
                             writes=[("ps", b)])
                        P.op("dve", lambda e, b=b, dc=dc, tt=tt: e.scalar_tensor_tensor(
                            out=hT[:, dc, tcols(tt)], in0=ps[b][:, :], scalar=0.5, in1=hT[:, dc, tcols(tt)],
                            op0=ALU.mult, op1=ALU.add), reads=[("ps", b), ("hT", dc, tt)],
                            writes=[("hT", dc, tt)])
                if after_half is not None:
                    after_half(tts, sq, rstd)

        ffn_calls = [0]

        def ffn_phase(li, gi, final=False):
            ffn_calls[0] += 1
            u_ = ffn_calls[0]
            with ExitStack() as ph:
                def psb(name, shape, dt=F32):
                    return ph.enter_context(nc.sbuf_tensor(name, list(shape), dt))
                HT = psb(f"ffnHT{li}_{u_}", [128, NFC, 1024], BF16)
                wA = [psb(f"fwA{li}_{i}_{u_}", [128, NDC, 256], BF16) for i in range(3)]
                wB = [psb(f"fwB{li}_{i}_{u_}", [128, NDC, 256], BF16) for i in range(3)]
                wD = [psb(f"fwD{li}_{i}_{u_}", [128, NFC, 128], BF16) for i in range(2)]
                fsq = psb(f"fsq{li}_{u_}", [128, NDC, 512], BF16)
                frstd = [psb(f"frstd{li}_{i}_{u_}", [128, 512]) for i in range(2)]
                ftmpf = [psb(f"ftmpf{li}_{i}_{u_}", [128, 512]) for i in range(4)]
                if final:
                    ost = [psb(f"ost{i}", [128, 512]) for i in range(2)]

                    def fin_store(dc, tt, dres):
                        o = dres[1]
                        P.op("sp", lambda e, dc=dc, tt=tt, o=o: e.dma_start(out=outT[dc][:, tcols(tt)], in_=ost[o][:, :]),
                             reads=[dres], dma=f"out{o}")

                    def after_half(tts, sq, rstd):
                        rmsnorm((sq, rstd), 3, lambda dc, tt: ost[(dc + tt * NDC) % 2][:, :],
                                lambda dc, tt: ("ost", (dc + tt * NDC) % 2), after_fn=fin_store, tts=tts)
                    ffn(li, gi, HT, wA, wB, wD, fsq, frstd, ftmpf, after_half=after_half)
                else:
                    ffn(li, gi, HT, wA, wB, wD, fsq, frstd, ftmpf)
                P.barrier()

        def proj_phase(v, own, QT, qmT, g_ext):
          with ExitStack() as ph:
            def psb(name, shape, dt=F32):
                return ph.enter_context(nc.sbuf_tensor(name, list(shape), dt))
            ropeC_s = psb(f"ropeC_s_{v}", [128, T])
            ropeS_s = psb(f"ropeS_s_{v}", [128, T])
            wA = [psb(f"mwA{i}_{v}", [128, NDC, 256], BF16) for i in range(2)]
            wB = [psb(f"mwB{i}_{v}", [128, NDC, 256], BF16) for i in range(2)]
            wV = psb(f"mwV_{v}", [128, NDC, 512], BF16)
            kst = [psb(f"kst{i}_{v}", [128, T], BF16) for i in range(1)]
            p_sq = psb(f"p_sq_{v}", [128, NDC, 512], BF16)
            p_rstd = [psb(f"p_rstd{i}_{v}", [128, 512]) for i in range(2)]
            p_tmpf = [psb(f"p_tmpf{i}_{v}", [128, 512]) for i in range(4)]
            gtail = psb(f"gtail_{v}", [128, 2, NBLK, 32], BF16)
            P.op("sp", lambda e: e.dma_start(out=ropeC_s[:], in_=ropeC), writes=["ropeC"], dma="rc")
            P.op("sp", lambda e: e.dma_start(out=ropeS_s[:], in_=ropeS), writes=["ropeS"], dma="rs")
            P.op("pool", lambda e: e.dma_start(out=wV[:], in_=w_inv), writes=["wV"], dma="wv")
            rmsnorm((p_sq, p_rstd), 1, act_dst, act_res)

            ag_pending = []

            def flush_ag():
                while ag_pending:
                    hp_ = ag_pending.pop(0)
                    P.op("pool", lambda e, hp_=hp_: e.collective_compute(
                        "AllGather", ALU.bypass, replica_groups=[[0, 1, 2, 3], [4, 5, 6, 7]],
                        ins=[bKi[hp_].opt()], outs=[bKo[hp_].opt()]),
                        reads=[("bK", 2 * hp_), ("bK", 2 * hp_ + 1)], writes=[("bKo", hp_)], dma=f"agK{hp_}", dma_inc=1)

            def rope_group(base_pair, is_k, pre=None):
                for hp in range(2):
                    if pre is not None and hp == 0:
                        sa, sbb = pre
                    else:
                        sa = load_w(wA, "wA", w_inf[base_pair + hp])
                        sbb = load_w(wB, "wB", w_inf[base_pair + 2 + hp])
                    flush_ag()
                    for sub in range(2):
                        h = 2 * hp + sub
                        ks = 0 if is_k else None
                        for tt in range(NTT):
                            ba, bb = nbank(), nbank()
                            proj(wA[sa], ("wA", sa), sub, tt, ba)
                            proj(wB[sbb], ("wB", sbb), sub, tt, bb)
                            k1, k2 = rot("tmpf", 4), rot("tmpf", 4)
                            P.op("dve", lambda e, k1=k1, ba=ba, tt=tt: e.tensor_tensor(
                                out=p_tmpf[k1][:, :], in0=ps[ba][:, :], in1=ropeC_s[:, tcols(tt)], op=ALU.mult),
                                reads=[("ps", ba), "ropeC"], writes=[("tmpf", k1)])
                            P.op("dve", lambda e, k2=k2, bb=bb, tt=tt: e.tensor_tensor(
                                out=p_tmpf[k2][:, :], in0=ps[bb][:, :], in1=ropeS_s[:, tcols(tt)], op=ALU.mult),
                                reads=[("ps", bb), "ropeS"], writes=[("tmpf", k2)])
                            if is_k:
                                dst, dres = kst[ks][:, tcols(tt)], ("kst", ks, tt)
                            else:
                                dst, dres = QT[:, h, tcols(tt)], ("QT", h, tt)
                            P.op("dve", lambda e, k1=k1, k2=k2, dst=dst: e.tensor_tensor(
                                out=dst, in0=p_tmpf[k1][:, :], in1=p_tmpf[k2][:, :], op=ALU.add),
                                reads=[("tmpf", k1), ("tmpf", k2)], writes=[dres])
                        if is_k:
                            P.op("sp", lambda e, ks=ks, h=h: e.dma_start(
                                out=bKi[h // 2][:, (h % 2) * T:(h % 2 + 1) * T], in_=kst[ks][:, :]),
                                reads=[("kst", ks, tt) for tt in range(NTT)], writes=[("bK", h)], dma=f"kst{ks}")
                            if False:
                                ag_pending.append(hp)
            rope_group(4, True)
            flush_ag()
            glu_pre = (load_w(wA, "wA", w_inf[8]), load_w(wB, "wB", w_inf[9]))
            bVx = [bVi[i].rearrange("p (h x) -> p h x", h=2) for i in range(2)]
            vst = [p_sq[:, 0:4, :], p_sq[:, 4:8, :]]
            vst4 = [v_.rearrange("p h (m e) -> p h m e", e=128) for v_ in vst]
            for tt in range(NTT):
                s = tt % 2
                for mi in range(4):
                    m = 4 * tt + mi
                    b = nbank()

                    def mm(e, m=m, b=b):
                        for dc in range(NDC):
                            i = e.matmul(ps[b][:, :], lhsT=actT[:, dc, m * 128:(m + 1) * 128], rhs=wV[:, dc, :],
                                         start=(dc == 0), stop=(dc == NDC - 1))
                        return i
                    P.op("pe", mm, reads=["wV"] + [("act", dc, tt) for dc in range(NDC)], writes=[("ps", b)])
                    P.op("act", lambda e, b=b, s=s, mi=mi: e.activation(out=vst4[s][:, :, mi, :], in_=v3(ps[b][:, :]), func=AF.Copy),
                         reads=[("ps", b)], writes=[("vst", s, mi)] + [("sq", dc) for dc in range(NDC)])
                for i_ in range(2):
                    P.op("sp", lambda e, s=s, tt=tt, i_=i_: e.dma_start(
                        out=bVx[i_][:, :, tt * 512:(tt + 1) * 512], in_=vst[s][:, 2 * i_:2 * i_ + 2, :]),
                        reads=[("vst", s, mi) for mi in range(4)], writes=[("bV", tt, i_)], dma=f"vst{s}")
            if own:
                q_pre = (load_w(wA, "wA", w_inf[0]), load_w(wB, "wB", w_inf[2]))
            for i_ in range(0):
                P.op("pool", lambda e, i_=i_: e.collective_compute(
                    "AllGather", ALU.bypass, replica_groups=[[0, 1, 2, 3], [4, 5, 6, 7]], ins=[bVi[i_].opt()], outs=[bVo[i_].opt()]),
                    reads=[("bV", tt, i_) for tt in range(NTT)], writes=[("bVo", i_)], dma=f"agV{i_}", dma_inc=1)
            sa, sbb = glu_pre
            for c in range(2):
                for tt in range(NTT):
                    ba, bb = nbank(), nbank()
                    proj(wA[sa], ("wA", sa), c, tt, ba)
                    proj(wB[sbb], ("wB", sbb), c, tt, bb)
                    k = rot("tmpf", 4)
                    P.op("act", lambda e, k=k, bb=bb: e.activation(
                        out=p_tmpf[k][:, :], in_=ps[bb][:, :], func=AF.Sigmoid),
                        reads=[("ps", bb)], writes=[("tmpf", k)])
                    P.op("dve", lambda e, k=k, ba=ba, c=c, tt=tt: e.tensor_tensor(
                        out=g_ext[:, c, 4 * tt:4 * tt + 4, 32:160], in0=v3(ps[ba][:, :]), in1=v3(p_tmpf[k][:, :]),
                        op=ALU.mult), reads=[("ps", ba), ("tmpf", k)], writes=[("g", c, tt)])
                P.op("dve", lambda e, c=c: e.tensor_copy(out=gtail[:, c, :, :], in_=g_ext[:, c, :, 128:160]),
                     reads=[("g", c, tt) for tt in range(NTT)], writes=[("gtail", c)])
            P.op("sp", lambda e: e.dma_start(
                out=bGi, in_=gtail[:].rearrange("p c m t -> p (c m t)")),
                reads=[("gtail", 0), ("gtail", 1)], writes=["bG"], dma="gt")
            P.op("pool", lambda e: e.collective_compute(
                "AllGather", ALU.bypass, replica_groups=[[0, 1, 2, 3], [4, 5, 6, 7]], ins=[bGi.opt()], outs=[bGo.opt()]),
                reads=["bG"], writes=["bGo"], dma="agG", dma_inc=1)
            if own:
                rope_group(0, False, pre=q_pre)
            if own:
                sa = load_w(wA, "wA", w_inf[10])
                for sub in range(2):
                    for tt in range(NTT):
                        b = nbank()
                        proj(wA[sa], ("wA", sa), sub, tt, b)
                        P.op("act", lambda e, b=b, sub=sub, tt=tt: e.activation(
                            out=qmT[:, sub, tcols(tt)], in_=ps[b][:, :], func=AF.Copy),
                            reads=[("ps", b)], writes=[("qmT", sub, tt)])
            P.barrier(exclude=AGS)


        def load_x(v):
            for tt in range(NTT):
                for dc in range(NDC):
                    P.op("sp", lambda e, dc=dc, tt=tt: e.dma_start(out=hT[:, dc, tcols(tt)], in_=xT[dc][:, tcols(tt)]),
                         writes=[("hT", dc, tt)], dma=f"x_t{tt}")

        load_x(0)
        if stage & 1:
            ffn_phase(0, 0)

        if stage & 2:
          with ExitStack() as mid1:
            QT = mid1.enter_context(nc.sbuf_tensor("QT", [128, 4, T], BF16))
            with ExitStack() as mid2:
                qmT = mid2.enter_context(nc.sbuf_tensor("qmT", [128, 2, T], BF16))
                g_ext = mid2.enter_context(nc.sbuf_tensor("g_ext", [128, 2, NBLK, 160], BF16))
                proj_phase(0, True, QT, qmT, g_ext)
                with ExitStack() as ph:
                    def psb(name, shape, dt=F32):
                        return ph.enter_context(nc.sbuf_tensor(name, list(shape), dt))
                    mem_s = psb("mem_s", [128, NDC, 256])
                    m_sq = psb("m_sq", [128, NDC, 256], BF16)
                    m_rstd = psb("m_rstd", [128, 256])
                    memn = psb("memn", [128, NDC, 256], BF16)
                    wkv = psb("wkv", [128, NDC, 512], BF16)
                    mkT = psb("mkT", [128, 2, 256], BF16)
                    mv = psb("mv", [128, 2, 4, 66], BF16)
                    pm = [psb(f"pm{i}", [128, 512], BF16) for i in range(4)]
                    om = [psb(f"om{i}", [128, 256]) for i in range(2)]
                    rl = [psb(f"rl{i}", [128, 4]) for i in range(2)]
                    tl_s = psb("tl_s", [128, 4, 2 * NBLK * 32], BF16)
                    hal = psb("hal", [128, 2, NBLK, 32])
                    dg = psb("dg", [128, 62, 128], BF16)
                    ysb = [psb(f"ysb{i}", [128, 512]) for i in range(2)]
                    ybf = [psb(f"ybf{i}", [128, 512], BF16) for i in range(2)]
                    ysq = [psb(f"ysq{i}", [128, 512], BF16) for i in range(2)]
                    mu_s = psb("mu_s", [128, 512])
                    var_s = psb("var_s", [128, 512])
                    if stage & 8:
                        for i in range(62):
                            eng = "dve"
                            P.op(eng, lambda e, i=i: e.tensor_scalar(
                                out=dg[:, i, :], in0=ident_bf[:, :], scalar1=smalls_s[:, i:i + 1], scalar2=None, op0=ALU.mult),
                                reads=["ident_bf", "smalls"], writes=[("dg", i)])

                    if stage & 4:
                        P.op("sp", lambda e: e.dma_start(out=mem_s[:], in_=memT.rearrange("c p t -> p c t")),
                             writes=["mem_s"], dma="mem")
                        P.op("pool", lambda e: e.dma_start(out=wkv[:], in_=w_kv), writes=["wkv"], dma="wkv")
                        P.op("pool", lambda e: e.collective_compute(
                            "AllGather", ALU.bypass, replica_groups=[[0, 1, 2, 3], [4, 5, 6, 7]],
                            ins=[bKi[0].opt()], outs=[bKo[0].opt()]),
                            reads=[("bK", 0), ("bK", 1), "wkv", "mem_s"], writes=[("bKo", 0)], dma="agK0", dma_inc=1)
                        P.op("pool", lambda e: e.collective_compute(
                            "AllGather", ALU.bypass, replica_groups=[[0, 1, 2, 3], [4, 5, 6, 7]],
                            ins=[bVi[0].opt()], outs=[bVo[0].opt()]),
                            reads=[("bV", tt_, 0) for tt_ in range(NTT)], writes=[("bVo", 0)], dma="agV0", dma_inc=1)
                        for dc in range(NDC):
                            P.op("act", lambda e, dc=dc: e.activation(out=m_sq[:, dc, :], in_=mem_s[:, dc, :], func=AF.Square),
                                 reads=["mem_s"], writes=[("sq", dc)])
                        b = nbank()

                        def mm(e, b=b):
                            for dc in range(NDC):
                                i = e.matmul(ps[b][:, 0:256], lhsT=ones_bf[:, :], rhs=m_sq[:, dc, :],
                                             start=(dc == 0), stop=(dc == NDC - 1))
                            return i
                        P.op("pe", mm, reads=[("sq", dc) for dc in range(NDC)] + ["ones"], writes=[("ps", b)])
                        rsqrt_act(m_rstd[:, :], ps[b][:, 0:256], 1.0, [("ps", b)], [("rstd", 0)])
                        for dc in range(NDC):
                            P.op("dve", lambda e, dc=dc: e.scalar_tensor_tensor(
                                out=memn[:, dc, :], in0=mem_s[:, dc, :], scalar=gains_s[:, 4, dc:dc + 1],
                                in1=m_rstd[:, :], op0=ALU.mult, op1=ALU.mult),
                                reads=["mem_s", ("rstd", 0), "gains"], writes=[("memn", dc)])
                        memn_res = [("memn", dc) for dc in range(NDC)]
                        for c2 in range(2):
                            b = nbank()

                            def mm(e, c2=c2, b=b):
                                for dc in range(NDC):
                                    i = e.matmul(ps[b][:, 0:256], lhsT=wkv[:, dc, c2 * 128:(c2 + 1) * 128], rhs=memn[:, dc, :],
                                                 start=(dc == 0), stop=(dc == NDC - 1))
                                return i
                            P.op("pe", mm, reads=["wkv"] + memn_res, writes=[("ps", b)])
                            P.op("act", lambda e, c2=c2, b=b: e.activation(out=mkT[:, c2, :], in_=ps[b][:, 0:256], func=AF.Copy),
                                 reads=[("ps", b)], writes=[("mkT", c2)])
                        P.op("dve", lambda e: e.memset(mv[:], 1.0), writes=["mv0", "mv1"])
                        for mb in range(2):
                            b = nbank()

                            def mm(e, mb=mb, b=b):
                                for dc in range(NDC):
                                    i = e.matmul(ps[b][:, 0:256], lhsT=memn[:, dc, mb * 128:(mb + 1) * 128], rhs=wkv[:, dc, 256:512],
                                                 start=(dc == 0), stop=(dc == NDC - 1))
                                return i
                            P.op("pe", mm, reads=["wkv"] + memn_res, writes=[("ps", b)])
                            P.op("act", lambda e, mb=mb, b=b: e.activation(
                                out=mv[:, mb, :, 0:64], in_=v3(ps[b][:, 0:256], 64), func=AF.Copy),
                                reads=[("ps", b), f"mv{mb}"], writes=[f"mv{mb}"])
                        for tt in range(NTT):
                            bO = (0, 1, 2, 3)

                            def acc(qb, bO=bO):
                                return v3(ps[bO[qb]][:, 0:264], 66)
                            for h in range(4):
                                c2, base = h // 2, (h % 2) * 64
                                pk = []
                                for mb in range(2):
                                    bS = 4 + rot("mbank", 4)
                                    P.op("pe", lambda e, bS=bS, c2=c2, base=base, mb=mb, tt=tt: e.matmul(
                                        ps[bS][:, :], lhsT=mkT[base:base + 64, c2, mb * 128:(mb + 1) * 128],
                                        rhs=qmT[base:base + 64, c2, tcols(tt)], start=True, stop=True),
                                        reads=[("mkT", c2), ("qmT", c2, tt)], writes=[("ps", bS)])
                                    k = rot("pm", 4)
                                    pk.append(k)
                                    P.op("act", lambda e, k=k, bS=bS: e.activation(
                                        out=pm[k][:, :], in_=ps[bS][:, :], func=AF.Exp, scale=0.125),
                                        reads=[("ps", bS)], writes=[("pm", k)])

                                def mm(e, h=h, pk=tuple(pk), acc=acc):
                                    for qb in range(4):
                                        for mb in range(2):
                                            i = e.matmul(acc(qb)[:, h, 0:66], lhsT=pm[pk[mb]][:, qb * 128:(qb + 1) * 128],
                                                         rhs=mv[:, mb, h, 0:66], start=(mb == 0), stop=(mb == 1))
                                    return i
                                P.op("pe", mm, reads=[("pm", pk[0]), ("pm", pk[1]), "mv0", "mv1"],
                                     writes=[("ps", bO[i_]) for i_ in range(4)])
                            for qb in range(4):
                                o = rot("om", 2)
                                P.op("dve", lambda e, o=o, qb=qb, acc=acc: e.reciprocal(out=rl[o][:, :], in_=acc(qb)[:, :, 64]),
                                     reads=[("ps", bO[qb])], writes=[("rl", o)])
                                for h in range(4):
                                    P.op("dve", lambda e, o=o, qb=qb, h=h, acc=acc: e.tensor_scalar(
                                        out=om[o][:, h * 64:(h + 1) * 64], in0=acc(qb)[:, h, 0:64], scalar1=rl[o][:, h:h + 1],
                                        scalar2=None, op0=ALU.mult),
                                        reads=[("ps", bO[qb]), ("rl", o)], writes=[("om", o, h)])
                                bT = 4 + rot("mbank", 4)

                                def tr(e, o=o, bT=bT):
                                    for c2 in range(2):
                                        i = e.transpose(out=ps[bT][:, c2 * 128:(c2 + 1) * 128], in_=om[o][:, c2 * 128:(c2 + 1) * 128],
                                                        identity=ident[:, :])
                                    return i
                                P.op("pe", tr, reads=[("om", o, h) for h in range(4)] + ["ident"], writes=[("ps", bT)])
                                cs = slice(tt * 512 + qb * 128, tt * 512 + (qb + 1) * 128)
                                P.op("act", lambda e, bT=bT, cs=cs: e.activation(
                                    out=actT[:, 6:8, cs], in_=v3(ps[bT][:, 0:256]), func=AF.Copy),
                                    reads=[("ps", bT)], writes=[("act", 6, tt), ("act", 7, tt)])
                    else:
                        P.op("dve", lambda e: e.memset(actT[:, 6:8, :], 0.0),
                             writes=[("act", d_, t_) for d_ in (6, 7) for t_ in range(NTT)])
                    if stage & 8:
                        for r in range(4):
                            P.op("sp", lambda e, r=r: e.dma_start(
                                out=tl_s[:, r, :], in_=bGo[r * 128:(r + 1) * 128, :]),
                                reads=["bGo"], writes=[("tl", r)], dma="tl")
                        tl5 = tl_s[:].rearrange("p r (c m t) -> p r c m t", c=2, m=NBLK)
                        for c in range(2):
                            P.op("dve", lambda e, c=c: e.tensor_scalar(
                                out=hal[:, c], in0=tl5[:, 0, c], scalar1=smalls_s[:, 80:81], scalar2=None, op0=ALU.mult),
                                reads=[("tl", 0), "smalls"], writes=[("hal", c)])
                            for r in (1, 2, 3):
                                P.op("dve", lambda e, c=c, r=r: e.scalar_tensor_tensor(
                                    out=hal[:, c], in0=tl5[:, r, c], scalar=smalls_s[:, 80 + r:81 + r], in1=hal[:, c],
                                    op0=ALU.mult, op1=ALU.add), reads=[("tl", r), ("hal", c)], writes=[("hal", c)])
                            for r in range(4):
                                P.op("dve", lambda e, c=c, r=r: e.scalar_tensor_tensor(
                                    out=hal[:, c, 1:NBLK, :], in0=tl5[:, r, c, 0:NBLK - 1, :], scalar=smalls_s[:, 84 + r:85 + r],
                                    in1=hal[:, c, 1:NBLK, :], op0=ALU.mult, op1=ALU.add),
                                    reads=[("tl", r), ("hal", c)], writes=[("hal", c)])
                            P.op("dve", lambda e, c=c: e.tensor_copy(out=g_ext[:, c, :, 0:32], in_=hal[:, c]),
                                 reads=[("hal", c)], writes=[("gh", c)])
                        for tt in range(NTT):
                            for c in range(2):
                                b = nbank()

                                def mm(e, c=c, tt=tt, b=b):
                                    for w in range(31):
                                        i = e.matmul(v3(ps[b][:, :]), lhsT=dg[:, c * 31 + w, :],
                                                     rhs=g_ext[:, c, 4 * tt:4 * tt + 4, w + 2:w + 130],
                                                     start=(w == 0), stop=(w == 30))
                                    return i
                                P.op("pe", mm, reads=[("dg", c * 31 + w) for w in range(31)] + [("g", c, t_) for t_ in range(NTT)] + [("gh", c)],
                                     writes=[("ps", b)])
                                P.op("act", lambda e, c=c, b=b: e.activation(
                                    out=ysb[c][:, :], in_=ps[b][:, :], func=AF.Identity, bias=smalls_s[:, 62 + c:63 + c]),
                                    reads=[("ps", b), "smalls"], writes=[("ysb", c)])
                                P.op("act", lambda e, c=c, b=b: e.activation(
                                    out=ysq[c][:, :], in_=ps[b][:, :], func=AF.Square, bias=smalls_s[:, 62 + c:63 + c]),
                                    reads=[("ps", b), "smalls"], writes=[("ysq", c)])
                                P.op("dve", lambda e, c=c: e.tensor_copy(out=ybf[c][:, :], in_=ysb[c][:, :]),
                                     reads=[("ysb", c)], writes=[("ybf", c)])
                            bmu, bm2 = nbank(), nbank()

                            def mm(e, bmu=bmu, bm2=bm2):
                                for c in range(2):
                                    e.matmul(ps[bmu][:, :], lhsT=ones256[:, :], rhs=ybf[c][:, :], start=(c == 0), stop=(c == 1))
                                for c in range(2):
                                    i = e.matmul(ps[bm2][:, :], lhsT=ones256[:, :], rhs=ysq[c][:, :], start=(c == 0), stop=(c == 1))
                                return i
                            P.op("pe", mm, reads=[("ybf", 0), ("ybf", 1), ("ysq", 0), ("ysq", 1), "ones256"],
                                 writes=[("ps", bmu), ("ps", bm2)])
                            P.op("act", lambda e, bmu=bmu: e.activation(out=mu_s[:, :], in_=ps[bmu][:, :], func=AF.Copy),
                                 reads=[("ps", bmu)], writes=["mu_s"])
                            P.op("dve", lambda e: e.tensor_tensor(out=var_s[:, :], in0=mu_s[:, :], in1=mu_s[:, :], op=ALU.mult),
                                 reads=["mu_s"], writes=["var_s"])
                            P.op("dve", lambda e, bm2=bm2: e.tensor_tensor(out=var_s[:, :], in0=ps[bm2][:, :], in1=var_s[:, :], op=ALU.subtract),
                                 reads=[("ps", bm2), "var_s"], writes=["var_s"])
                            rsqrt_act(var_s[:, :], var_s[:, :], 1.0, ["var_s"], ["var_s"])
                            for c in range(2):
                                P.op("dve", lambda e, c=c: e.tensor_tensor(out=ysb[c][:, :], in0=ysb[c][:, :], in1=mu_s[:, :], op=ALU.subtract),
                                     reads=[("ysb", c), "mu_s", ("ybf", c)], writes=[("ysb", c)])
                                P.op("dve", lambda e, c=c: e.tensor_tensor(out=ysb[c][:, :], in0=ysb[c][:, :], in1=var_s[:, :], op=ALU.mult),
                                     reads=[("ysb", c), "var_s"], writes=[("ysb", c)])
                                P.op("act", lambda e, c=c, tt=tt: e.activation(
                                    out=actT[:, 4 + c, tcols(tt)], in_=ysb[c][:, :], func=AF.Silu,
                                    scale=smalls_s[:, 64 + c:65 + c], bias=smalls_s[:, 66 + c:67 + c]),
                                    reads=[("ysb", c), "smalls"], writes=[("act", 4 + c, tt)])
                    else:
                        P.op("dve", lambda e: e.memset(actT[:, 4:6, :], 0.0),
                             writes=[("act", d_, t_) for d_ in (4, 5) for t_ in range(NTT)])
                    P.barrier()

            with ExitStack() as ph:
                def psb(name, shape, dt=F32):
                    return ph.enter_context(nc.sbuf_tensor(name, list(shape), dt))
                if stage & 16:
                    KT = [psb(f"KT{i}", [128, 4, T], BF16) for i in range(2)]
                    VA = [psb(f"VA{i}", [128, 4 * NBLK, 132], BF16) for i in range(2)]
                    PT = [psb(f"PT{i}", [128, 2, 512], BF16) for i in range(3)]
                    maskb = psb("maskb", [128, 4, 128], BF16)
                    Osb = [psb(f"Osb{i}", [128, 8, 130]) for i in range(2)]
                    cmb = [psb(f"cmb{i}", [128, 8]) for i in range(4)]
                    ot = [psb(f"ot{i}", [128, 128]) for i in range(4)]
                    oo = [psb(f"oo{i}", [128, 128]) for i in range(4)]
                    osq = [psb(f"osq{i}", [128, 128]) for i in range(4)]
                    P.op("pool", lambda e: e.dma_start(out=maskb[:], in_=maskd.rearrange("r k q -> k r q")),
                         writes=["maskb"], dma="mask")
                    for s_ in range(2):
                        P.op("dve", lambda e, s_=s_: e.memset(VA[s_][:, :, 128:130], 1.0), writes=[("va1", s_)])
                    zr = psb("zr", [128, 512], BF16)
                    P.op("dve", lambda e: e.memset(zr[:], 0.0), writes=["zr"])

                    def zero_acc(e):
                        for bi, n in ((0, 3), (1, 3), (2, 2)):
                            i = e.matmul(ps[4 + bi][:, 0:130 * n], lhsT=zr[:, 0:128], rhs=zr[:, 0:130 * n], start=True, stop=False)
                        return i

                    def accO(i):
                        return ps[4 + i // 3][:, (i % 3) * 130:(i % 3) * 130 + 130]
                    pending = []

                    def flush_pending():
                        for f in pending:
                            f()
                        pending.clear()
                    for h in range(4):
                        s = h % 2
                        for r in range(4):
                            P.op("sp", lambda e, s=s, r=r, h=h: e.dma_start(
                                out=KT[s][:, r, :], in_=bKo[h // 2][r * 128:(r + 1) * 128, (h % 2) * T:(h % 2 + 1) * T]),
                                reads=[("bKo", h // 2)], writes=[("kt", s, r)], dma=f"kt{s}_{r}")
                            P.op("sp", lambda e, s=s, r=r, h=h: e.dma_start(
                                out=VA[s][:, r * NBLK:(r + 1) * NBLK, 0:128],
                                in_=v3(bVo[h // 2][r * 128:(r + 1) * 128, (h % 2) * T:(h % 2 + 1) * T])),
                                reads=[("bVo", h // 2)], writes=[("va", s, r)], dma=f"va{s}_{r}")
                        if h == 0:
                            P.op("pool", lambda e: e.collective_compute(
                                "AllGather", ALU.bypass, replica_groups=[[0, 1, 2, 3], [4, 5, 6, 7]],
                                ins=[bKi[1].opt()], outs=[bKo[1].opt()]),
                                reads=[("bK", 2), ("bK", 3)] + [("kt", 0, r_) for r_ in range(4)] + [("va", 0, r_) for r_ in range(4)],
                                writes=[("bKo", 1)], dma="agK1", dma_inc=1)
                            P.op("pool", lambda e: e.collective_compute(
                                "AllGather", ALU.bypass, replica_groups=[[0, 1, 2, 3], [4, 5, 6, 7]],
                                ins=[bVi[1].opt()], outs=[bVo[1].opt()]),
                                reads=[("bV", tt_, 1) for tt_ in range(NTT)], writes=[("bVo", 1)], dma="agV1", dma_inc=1)
                        for qg in range(4):
                            pairs = [(r, mp) for r in range(4) for mp in range(4 * qg + 4)]
                            npairs = len(pairs)
                            info = []

                            def emit_qk(idx, r, mp):
                                spb = idx % 2
                                c0 = max(mp - 4 * qg, 0)

                                def mm(e, spb=spb, c0=c0, r=r, mp=mp, s=s, h=h, qg=qg):
                                    for comp in range(2):
                                        i = e.matmul(ps[2 * spb + comp][:, c0 * 128:512],
                                                     lhsT=KT[s][comp * 64:(comp + 1) * 64, r, mp * 128:(mp + 1) * 128],
                                                     rhs=QT[comp * 64:(comp + 1) * 64, h, qg * 512 + c0 * 128:(qg + 1) * 512],
                                                     start=True, stop=True)
                                    return i
                                P.op("pe", mm, reads=[("kt", s, r), ("QT", h, qg)], writes=[("ps", 2 * spb), ("ps", 2 * spb + 1)])
                                k = rot("PT", 3)

                                def ex(e, spb=spb, c0=c0, k=k):
                                    return e.activation(out=PT[k][:, :, c0 * 128:512],
                                                        in_=psS[spb][:, :].rearrange("p (c n) -> p c n", c=2)[:, :, c0 * 128:512],
                                                        func=AF.Exp, scale=0.125)
                                P.op("act", ex, reads=[("ps", 2 * spb), ("ps", 2 * spb + 1)], writes=[("PT", k)])
                                if mp >= 4 * qg:
                                    def mk(e, c0=c0, k=k, r=r):
                                        for comp in range(2):
                                            i = e.tensor_tensor(out=PT[k][:, comp, c0 * 128:(c0 + 1) * 128],
                                                                in0=PT[k][:, comp, c0 * 128:(c0 + 1) * 128],
                                                                in1=maskb[:, r, :], op=ALU.mult)
                                        return i
                                    P.op("dve", mk, reads=[("PT", k), "maskb"], writes=[("PT", k)])
                                return (k, c0)

                            def emit_pv(idx, r, mp, k, c0):
                                def mm(e, r=r, mp=mp, k=k, c0=c0, qg=qg, s=s):
                                    for qb in range(c0, 4):
                                        for comp in range(2):
                                            i = e.matmul(accO(2 * qb + comp), lhsT=PT[k][:, comp, qb * 128:(qb + 1) * 128],
                                                         rhs=VA[s][:, r * NBLK + mp, 0:130],
                                                         start=False,
                                                         stop=(r == 3 and mp == 4 * qg + qb and (qb, comp) in ((1, 0), (2, 1), (3, 1))))
                                    return i
                                P.op("pe", mm, reads=[("PT", k), ("va", s, r), ("va1", s)],
                                     writes=[("ps", 4), ("ps", 5), ("ps", 6)])
                            prev = None
                            P.op("pe", zero_acc, reads=["zr"], writes=[("ps", 4), ("ps", 5), ("ps", 6)])
                            for idx, (r, mp) in enumerate(pairs):
                                cur = emit_qk(idx, r, mp)
                                if prev is not None:
                                    emit_pv(*prev)
                                prev = (idx, r, mp, cur[0], cur[1])
                                if idx == 3:
                                    flush_pending()
                            emit_pv(*prev)
                            flush_pending()
                            osl = rot("Osb", 2)
                            for bi, n in ((0, 3), (1, 3), (2, 2)):
                                P.op("dve", lambda e, bi=bi, n=n, osl=osl: e.tensor_copy(
                                    out=Osb[osl][:, 3 * bi:3 * bi + n, :], in_=v3(ps[4 + bi][:, 0:130 * n], 130)),
                                    reads=[("ps", 4 + bi)], writes=[("Osb", osl, bi)])
                            ores = [("Osb", osl, bi) for bi in range(3)]
                            for qb in range(4):
                                o = qb
                                O = Osb[osl]
                                c8 = cmb[o]
                                P.op("dve", lambda e, O=O, c8=c8, qb=qb: e.reciprocal(out=c8[:, 0:2], in_=O[:, 2 * qb:2 * qb + 2, 128]),
                                     reads=ores, writes=[("cmb", o)])
                                P.op("dve", lambda e, c8=c8: e.tensor_scalar(out=c8[:, 2:3], in0=c8[:, 1:2], scalar1=smalls_s[:, 74:75],
                                                                             scalar2=None, op0=ALU.mult),
                                     reads=[("cmb", o), "neglam"], writes=[("cmb", o)])
                                P.op("dve", lambda e, O=O, c8=c8, qb=qb, o=o: e.tensor_scalar(
                                    out=ot[o][:, :], in0=O[:, 2 * qb, 0:128], scalar1=c8[:, 0:1], scalar2=None, op0=ALU.mult),
                                    reads=ores + [("cmb", o)], writes=[("ot", o)])
                                P.op("dve", lambda e, O=O, c8=c8, qb=qb, o=o: e.scalar_tensor_tensor(
                                    out=oo[o][:, :], in0=O[:, 2 * qb + 1, 0:128], scalar=c8[:, 2:3], in1=ot[o][:, :],
                                    op0=ALU.mult, op1=ALU.add), reads=ores + [("cmb", o), ("ot", o)], writes=[("oo", o)])
                                P.op("dve", lambda e, o=o: e.tensor_tensor(out=osq[o][:, :], in0=oo[o][:, :], in1=oo[o][:, :], op=ALU.mult),
                                     reads=[("oo", o)], writes=[("osq", o)])
                                P.op("dve", lambda e, o=o, c8=c8: e.reduce_sum(out=c8[:, 3:4], in_=osq[o][:, :], axis=mybir.AxisListType.X),
                                     reads=[("osq", o), ("cmb", o)], writes=[("cmb", o)])
                                def fin(o=o, qb=qb, h=h, qg=qg, c8=c8):
                                    rsqrt_act(c8[:, 5:6], c8[:, 3:4], 1.0 / 128.0, [("cmb", o)], [("cmb5", o)])
                                    P.op("dve", lambda e: e.tensor_scalar(out=ot[o][:, :], in0=oo[o][:, :], scalar1=c8[:, 5:6],
                                                                          scalar2=None, op0=ALU.mult),
                                         reads=[("oo", o), ("cmb5", o), ("ot", o)], writes=[("ot", o)])
                                    P.op("pe", lambda e: e.transpose(out=ps[7][:, qb * 128:(qb + 1) * 128], in_=ot[o][:, :],
                                                                    identity=ident[:, :]),
                                         reads=[("ot", o), "ident"], writes=[("ps7", qb)])
                                    P.op("dve", lambda e: e.tensor_scalar(
                                        out=actT[:, h, qg * 512 + qb * 128:qg * 512 + (qb + 1) * 128],
                                        in0=ps[7][:, qb * 128:(qb + 1) * 128], scalar1=smalls_s[:, 75:76], scalar2=None, op0=ALU.mult),
                                        reads=[("ps7", qb), "sg8"], writes=[("act", h, qg)])
                                pending.append(fin)
                    flush_pending()
                else:
                    P.op("dve", lambda e: e.memset(actT[:, 0:4, :], 0.0),
                         writes=[("act", d_, t_) for d_ in range(4) for t_ in range(NTT)])
                P.barrier()

            with ExitStack() as ph:
                wA = [ph.enter_context(nc.sbuf_tensor(f"owA{i}", [128, NDC, 256], BF16)) for i in range(2)]
                for dcp in range(4):
                    sa = load_w(wA, "wA", w_o[dcp])
                    for sub in range(2):
                        dc = 2 * dcp + sub
                        for tt in range(NTT):
                            b = nbank()
                            proj(wA[sa], ("wA", sa), sub, tt, b)
                            P.op("dve", lambda e, b=b, dc=dc, tt=tt: e.tensor_tensor(
                                out=hT[:, dc, tcols(tt)], in0=ps[b][:, :], in1=hT[:, dc, tcols(tt)], op=ALU.add),
                                reads=[("ps", b), ("hT", dc, tt)], writes=[("hT", dc, tt)])
                P.barrier()

        if stage & 32:
            ffn_phase(1, 2, final=True)
        else:
            f_sq = sb("f_sq", [128, NDC, 512], BF16)
            f_rstd = [sb(f"f_rstd{i}", [128, 512]) for i in range(2)]
            ost = [sb(f"ost{i}", [128, 512]) for i in range(2)]

            def fin_store(dc, tt, dres):
                o = dres[1]
                P.op("sp", lambda e, dc=dc, tt=tt, o=o: e.dma_start(out=outT[dc][:, tcols(tt)], in_=ost[o][:, :]),
                     reads=[dres], dma=f"out{o}")
            rmsnorm((f_sq, f_rstd), 3, lambda dc, tt: ost[(dc + tt * NDC) % 2][:, :],
                    lambda dc, tt: ("ost", (dc + tt * NDC) % 2), after_fn=fin_store)
        P.barrier()
        P.emit()
    return nc


def _tok_idx(j):
    m = np.arange(NBLK)
    g = 4 * m + j
    return (g[:, None] * 128 + np.arange(128)[None, :]).reshape(-1)


def _pairs(w, ncol_chunks):
    return np.ascontiguousarray(
        w.reshape(NDC, 128, ncol_chunks, 256).transpose(2, 1, 0, 3))


def _prep(inputs):
    f = lambda k: np.asarray(inputs[k], dtype=np.float32)
    x = f("x")
    mem = f("mem")
    common = {}
    for i, pre in enumerate(("ffn1", "ffn2")):
        common[f"wg{i}"] = _pairs(f(pre + "_w_gate")[0], NFC // 2)
        common[f"wu{i}"] = _pairs(f(pre + "_w_up")[0], NFC // 2)
        wd = f(pre + "_w_down")[0]
        common[f"wd{i}"] = np.ascontiguousarray(
            wd.reshape(NFC, 128, NDC, 128).transpose(2, 1, 0, 3))
    w_in = f("w_in")[0]
    q, k, v, u, qm = (w_in[:, 0:512], w_in[:, 512:1024], w_in[:, 1024:1536],
                      w_in[:, 1536:2048], w_in[:, 2048:2304])

    def swap(w):
        w4 = w.reshape(D, 4, 2, 2, 32)
        return w4[:, :, :, ::-1, :].reshape(D, 512)
    wf = np.concatenate([q, swap(q), k, swap(k), u, qm], axis=1)
    assert wf.shape[1] == 11 * 256
    common["w_inf"] = _pairs(np.ascontiguousarray(wf), wf.shape[1] // 256)
    common["w_inv"] = np.ascontiguousarray(v.reshape(NDC, 128, 512).transpose(1, 0, 2))
    common["w_kv"] = np.ascontiguousarray(f("w_mem_kv")[0].reshape(NDC, 128, 512).transpose(1, 0, 2))
    common["w_o"] = _pairs(f("w_out")[0], 4)
    gains = np.stack([f("ffn1_norm_g")[0], f("mix_norm_g")[0], f("ffn2_norm_g")[0],
                      f("final_norm_g"), f("mem_norm_g")[0]], 0)
    common["gains"] = np.ascontiguousarray(gains.reshape(5, NDC, 128).transpose(2, 0, 1))
    common["ident"] = np.eye(128, dtype=np.float32)
    sm = np.zeros((128, 96), np.float32)
    dw = f("conv_dw_w")[0]
    sm[:, 0:62] = dw.reshape(31, 2, 128).transpose(2, 1, 0).reshape(128, 62)
    sm[:, 62:64] = f("conv_dw_b")[0].reshape(2, 128).T
    sm[:, 64:66] = f("conv_ln_g")[0].reshape(2, 128).T
    sm[:, 66:68] = f("conv_ln_b")[0].reshape(2, 128).T
    sm[:, 68] = f("subln_g")[0]
    common["smalls_base"] = sm
    lam = np.stack([f("lambda_q1")[0], f("lambda_q2")[0], f("lambda_k1")[0], f("lambda_k2")[0]], 0)
    common["lamv"] = np.ascontiguousarray(np.broadcast_to(lam[None], (128, 4, 64))).astype(np.float32)

    inv_freq = (1.0 / (np.float32(10000.0) ** (np.arange(0, 64, 2, dtype=np.float32) / np.float32(64)))).astype(np.float32)
    in_maps = []
    for c in range(NCORES):
        b, j = c // 4, c % 4
        m = dict((k2, v2) for k2, v2 in common.items() if k2 != "smalls_base")
        idx = _tok_idx(j)
        m["xT"] = np.ascontiguousarray(x[b][idx].T.reshape(NDC, 128, T))
        ang = (idx.astype(np.float32)[:, None] * inv_freq[None, :]).astype(np.float32)
        cos = np.cos(ang).astype(np.float32).T
        sin = np.sin(ang).astype(np.float32).T
        m["ropeC"] = np.ascontiguousarray(np.concatenate([cos, cos, cos, cos], 0))
        m["ropeS"] = np.ascontiguousarray(np.concatenate([-sin, sin, -sin, sin], 0))
        m["memT"] = np.ascontiguousarray(mem[b].T.reshape(NDC, 128, 256))
        sm2 = common["smalls_base"].copy()
        rs = (j - 1) % 4
        if j == 0:
            sm2[:, 84 + rs] = 1.0
        else:
            sm2[:, 80 + rs] = 1.0
        m["smalls"] = sm2
        md = np.zeros((4, 128, 128), np.float32)
        for r in range(4):
            if r < j:
                md[r] = 1.0
            elif r == j:
                kk = np.arange(128)[:, None]
                qq = np.arange(128)[None, :]
                md[r] = ((kk < 64) | (qq >= 64)).astype(np.float32)
        m["maskd"] = md
        in_maps.append(m)
    return in_maps


_NC_CACHE = {}


def kernel(**inputs):
    stage = int(inputs.pop("_stage", 63)) if "_stage" in inputs else 63
    in_maps = _prep(inputs)
    if stage not in _NC_CACHE:
        _NC_CACHE[stage] = build(stage)
    nc = _NC_CACHE[stage]
    res = run_bass_kernel_spmd(nc, in_maps, core_ids=list(range(NCORES)))
    out = np.zeros((2, SEQ, D), np.float32)
    for c in range(NCORES):
        b, j = c // 4, c % 4
        o = np.asarray(res.results[c]["outT"]).reshape(D, T)
        out[b][_tok_idx(j)] = o.T
    return out
```
